# Optimizing a Trainium2 kernel written in Bass

```python
import math
import jax, jax.numpy as jnp
from jax import lax
import numpy as np

D_MODEL = 1024
BATCH = 4
SEQ = 4096
DEPTH = 1

ATTN_HEAD_DIM = 64
ATTN_Q_HEADS = 16
ATTN_KV_HEADS = 2
ATTN_GROUP = ATTN_Q_HEADS // ATTN_KV_HEADS
ATTN_WIDTH = ATTN_Q_HEADS * ATTN_HEAD_DIM
ATTN_KV_WIDTH = ATTN_KV_HEADS * ATTN_HEAD_DIM
WINDOW = 128
ATTN_BLOCK = 128
HGRN_EXPAND = 128
HGRN_HEADS = D_MODEL // HGRN_EXPAND
HGRN_K = HGRN_EXPAND
HGRN_V = D_MODEL // HGRN_HEADS
HGRN_KEY_WIDTH = HGRN_HEADS * HGRN_K
HGRN_WIDTH = HGRN_HEADS * HGRN_V
CHUNK = 64
FFN_HIDDEN = -(-(8 * D_MODEL) // (3 * 256)) * 256
EPS = 1e-6
NEG_INF = -1e30
IN_SPLITS = (ATTN_WIDTH, ATTN_KV_WIDTH, ATTN_KV_WIDTH,
             HGRN_KEY_WIDTH, HGRN_KEY_WIDTH, HGRN_WIDTH, HGRN_WIDTH,
             D_MODEL, D_MODEL)
IN_WIDTH = sum(IN_SPLITS)

kernel_name = "hybrid_swa_sink_hgrn2_gated_block"


def rmsnorm(x, g):
    xf = x.astype(jnp.float32)
    y = xf * lax.rsqrt(jnp.mean(xf * xf, axis=-1, keepdims=True) + EPS)
    return (y * g.astype(jnp.float32)).astype(x.dtype)


def split_columns(p):
    idx = np.cumsum(np.array(IN_SPLITS))[:-1].tolist()
    return jnp.split(p, idx, axis=-1)


def sliding_window_sink_attention(q, k, v, sinks):
    B, S = q.shape[0], q.shape[1]
    nb = S // ATTN_BLOCK
    qb = q.reshape(B, nb, ATTN_BLOCK, ATTN_KV_HEADS, ATTN_GROUP, ATTN_HEAD_DIM)
    kb = k.reshape(B, nb, ATTN_BLOCK, ATTN_KV_HEADS, ATTN_HEAD_DIM)
    vb = v.reshape(B, nb, ATTN_BLOCK, ATTN_KV_HEADS, ATTN_HEAD_DIM)

    def with_prev(t):
        prev = jnp.concatenate([jnp.zeros_like(t[:, :1]), t[:, :-1]], axis=1)
        return jnp.concatenate([prev, t], axis=2)

    kw, vw = with_prev(kb), with_prev(vb)
    scale = 1.0 / math.sqrt(ATTN_HEAD_DIM)
    scores = jnp.einsum('bnqhgd,bnkhd->bnhgqk', qb, kw).astype(jnp.float32) * scale
    qi = jnp.arange(ATTN_BLOCK)[:, None]
    kj = jnp.arange(2 * ATTN_BLOCK)[None, :]
    rel = qi + ATTN_BLOCK - kj
    key_pos = jnp.arange(nb)[:, None, None] * ATTN_BLOCK - ATTN_BLOCK + kj[None]
    valid = (rel >= 0)[None] & (rel < WINDOW)[None] & (key_pos >= 0)
    scores = jnp.where(valid[None, :, None, None], scores, NEG_INF)
    sink = jnp.broadcast_to(
        sinks.astype(jnp.float32).reshape(1, 1, ATTN_KV_HEADS, ATTN_GROUP, 1, 1),
        scores.shape[:-1] + (1,))
    probs = jax.nn.softmax(jnp.concatenate([scores, sink], axis=-1), axis=-1)[..., :-1]
    out = jnp.einsum('bnhgqk,bnkhd->bnqhgd', probs.astype(v.dtype), vw)
    return out.reshape(B, S, ATTN_WIDTH)


def hgrn2_chunkwise(q, k, v, log_f):
    B, S, H, K = q.shape
    V = v.shape[-1]
    n = S // CHUNK

    def to_chunks(t):
        return t.reshape(B, n, CHUNK, H, t.shape[-1]).transpose(1, 0, 3, 2, 4)

    causal = jnp.tril(jnp.ones((CHUNK, CHUNK), dtype=bool))

    def step(state, inp):
        qc, kc, vc, gc = inp
        b = jnp.cumsum(gc, axis=2)
        b_mid = b[:, :, CHUNK // 2 - 1:CHUNK // 2]
        b_last = b[:, :, -1:]
        a = jnp.einsum('bhck,bhsk->bhcs', qc * jnp.exp(b - b_mid), kc * jnp.exp(b_mid - b))
        a = jnp.where(causal, a, 0.0)
        o = (jnp.einsum('bhcs,bhsv->bhcv', a, vc)
             + jnp.einsum('bhck,bhkv->bhcv', qc * jnp.exp(b), state))
        state = (jnp.exp(b_last)[:, :, 0, :, None] * state
                 + jnp.einsum('bhsk,bhsv->bhkv', kc * jnp.exp(b_last - b), vc))
        return state, o

    s0 = jnp.zeros((B, H, K, V), jnp.float32)
    _, o = lax.scan(step, s0, (to_chunks(q), to_chunks(k), to_chunks(v), to_chunks(log_f)))
    return o.transpose(1, 0, 3, 2, 4).reshape(B, S, H, V)


def hgrn2_branch(hq, hf, hi, hg, lb, norm_g):
    B, S = hq.shape[0], hq.shape[1]
    fp = hf.astype(jnp.float32)
    log_f = jnp.log(lb + (1.0 - lb) * jax.nn.sigmoid(fp))
    k = (1.0 - lb) * jax.nn.sigmoid(-fp)
    q = jax.nn.silu(hq.astype(jnp.float32))
    shp_k = (B, S, HGRN_HEADS, HGRN_K)
    o = hgrn2_chunkwise(q.reshape(shp_k), k.reshape(shp_k),
                        hi.astype(jnp.float32).reshape(B, S, HGRN_HEADS, HGRN_V),
                        log_f.reshape(shp_k))
    o = o * lax.rsqrt(jnp.mean(o * o, axis=-1, keepdims=True) + EPS)
    o = o.reshape(B, S, HGRN_WIDTH) * norm_g.astype(jnp.float32)
    o = o * jax.nn.sigmoid(hg.astype(jnp.float32))
    return o.astype(hq.dtype)


def setup_inputs(seed: int = 0) -> dict:
    key = jax.random.key(seed)
    ks = jax.random.split(key, 20)
    D, F = D_MODEL, FFN_HIDDEN
    nrm = lambda k, shape, fan_in: jax.random.normal(k, shape, jnp.float32) * fan_in ** -0.5
    gain = lambda k, shape: 1.0 + 0.05 * jax.random.normal(k, shape, jnp.float32)
    return {
        "x": jax.random.normal(ks[0], (BATCH, SEQ, D), jnp.float32),
        "norm_mix_g": gain(ks[1], (DEPTH, D)),
        "w_in": nrm(ks[2], (DEPTH, D, IN_WIDTH), D),
        "b_in": 0.01 * jax.random.normal(ks[3], (DEPTH, IN_WIDTH), jnp.float32),
        "attn_sinks": 0.5 * jax.random.normal(ks[4], (DEPTH, ATTN_Q_HEADS), jnp.float32),
        "hgrn_lb_logits": gain(ks[5], (DEPTH + 1, HGRN_KEY_WIDTH)),
        "hgrn_norm_g": gain(ks[6], (DEPTH, HGRN_WIDTH)),
        "w_branch_attn": nrm(ks[7], (DEPTH, ATTN_WIDTH, D), ATTN_WIDTH),
        "w_branch_hgrn": nrm(ks[8], (DEPTH, HGRN_WIDTH, D), HGRN_WIDTH),
        "w_out": nrm(ks[9], (DEPTH, D, D), D),
        "norm_ffn_g": gain(ks[10], (DEPTH, D)),
        "w_ffn_gate": nrm(ks[11], (DEPTH, D, F), D),
        "w_ffn_up": nrm(ks[12], (DEPTH, D, F), D),
        "w_ffn_down": nrm(ks[13], (DEPTH, F, D), F),
        "norm_final_g": gain(ks[14], (D,)),
    }


def reference(x, norm_mix_g, w_in, b_in, attn_sinks, hgrn_lb_logits, hgrn_norm_g,
              w_branch_attn, w_branch_hgrn, w_out, norm_ffn_g, w_ffn_gate, w_ffn_up,
              w_ffn_down, norm_final_g):
    lb_all = jnp.cumsum(jax.nn.softmax(hgrn_lb_logits.astype(jnp.float32), axis=0), axis=0)
    h = x
    for layer in range(DEPTH):
        u = rmsnorm(h, norm_mix_g[layer])
        p = jnp.einsum('bsd,de->bse', u, w_in[layer]) + b_in[layer]
        aq, ak, av, hq, hf, hi, hg, gate_a, gate_b = split_columns(p)
        y_attn = sliding_window_sink_attention(aq, ak, av, attn_sinks[layer])
        y_hgrn = hgrn2_branch(hq, hf, hi, hg, lb_all[layer], hgrn_norm_g[layer])
        ya = jnp.einsum('bse,ed->bsd', y_attn, w_branch_attn[layer])
        yb = jnp.einsum('bse,ed->bsd', y_hgrn, w_branch_hgrn[layer])
        merged = jax.nn.sigmoid(gate_a) * ya + jax.nn.sigmoid(gate_b) * yb
        h = h + jnp.einsum('bsd,de->bse', merged, w_out[layer])
        u = rmsnorm(h, norm_ffn_g[layer])
        z = (jax.nn.silu(jnp.einsum('bsd,df->bsf', u, w_ffn_gate[layer]))
             * jnp.einsum('bsd,df->bsf', u, w_ffn_up[layer]))
        h = h + jnp.einsum('bsf,fd->bsd', z, w_ffn_down[layer])
    return rmsnorm(h, norm_final_g)
```

```python
import numpy as np
from contextlib import ExitStack
import concourse.bass as bass
import concourse.mybir as mybir
from concourse.bass_utils import run_bass_kernel_spmd

F32 = mybir.dt.float32
BF16 = mybir.dt.bfloat16
AF = mybir.ActivationFunctionType
ALU = mybir.AluOpType

D = 1024
KC = 8
INW = 7424
C_AQ, C_AK, C_AV, C_HQ, C_HF, C_HI, C_HG, C_GA, C_GB = 0, 1024, 1152, 1280, 2304, 3328, 4352, 5376, 6400
FF = 2816
FC = 22
EPS = 1e-6
ARENA_KB = 203


class StopBuild(Exception):
    pass


class _Op:
    __slots__ = ("eng", "emit", "deps", "sig", "signal", "dma_sem", "dma_val", "is_dma", "ndma")


class Prog:
    ENGS = ("pe", "act", "dve", "pool", "sp")

    def __init__(self, nc):
        self.nc = nc
        self.ops = []
        self.last_w = {}
        self.readers = {}
        self.dma_sems = {}
        self.last_eng = {}
        self.last_dma = {}
        self.bar = {}
        self.limit = None

    def _mk(self, eng, emit, r, w, extra):
        if self.limit is not None and len(self.ops) >= self.limit:
            return None
        op = _Op()
        op.eng = eng
        op.emit = emit
        op.sig = False
        op.signal = 0
        op.is_dma = False
        op.dma_sem = None
        op.dma_val = 0
        op.ndma = 1
        deps = set(extra)
        b = self.bar.pop(eng, None)
        if b:
            deps.update(b)
        for k in r:
            lw = self.last_w.get(k)
            if lw is not None:
                deps.add(lw)
        for k in w:
            lw = self.last_w.get(k)
            if lw is not None:
                deps.add(lw)
            for rd in self.readers.get(k, ()):
                deps.add(rd)
        deps.discard(op)
        if eng == "pe":
            deps = {d for d in deps if d.is_dma or d.eng != "pe"}
        op.deps = deps
        for k in r:
            self.readers.setdefault(k, []).append(op)
        for k in w:
            self.last_w[k] = op
            self.readers[k] = []
        self.ops.append(op)
        return op

    def op(self, eng, emit, r=(), w=(), deps=()):
        op = self._mk(eng, emit, r, w, deps)
        if op is None:
            return None
        self.last_eng[eng] = op
        return op

    def dma(self, eng, emit, slot, r=(), w=(), deps=(), n=1):
        op = self._mk(eng, emit, r, w, deps)
        if op is None:
            return None
        op.is_dma = True
        op.ndma = n
        st = self.dma_sems.setdefault(slot, [None, 0])
        st[1] += 16 * n
        op.dma_sem = slot
        op.dma_val = st[1]
        self.last_dma[slot] = op
        return op

    def barrier(self):
        allp = list(self.last_eng.values()) + list(self.last_dma.values())
        for e in self.ENGS:
            self.bar[e] = list(allp)

    def emit_all(self, block, stack):
        nc = self.nc
        ops = self.ops
        for op in ops:
            for d in op.deps:
                if not d.is_dma:
                    d.sig = True
        cnt = {e: 0 for e in self.ENGS}
        for op in ops:
            if op.sig and not op.is_dma:
                cnt[op.eng] += 1
                op.signal = cnt[op.eng]
        esem = {e: stack.enter_context(nc.semaphore("c_" + e)) for e in self.ENGS}
        for slot, st in self.dma_sems.items():
            st[0] = stack.enter_context(nc.semaphore("d_" + str(slot)))
        by_eng = {e: [o for o in ops if o.eng == e] for e in self.ENGS}

        def run(engname, e):
            waited = {}
            for op in by_eng[engname]:
                need = {}
                for d in op.deps:
                    if d.is_dma:
                        key = ("d", d.dma_sem)
                        val = d.dma_val
                    else:
                        key = ("c", d.eng)
                        val = d.signal
                    if val > need.get(key, 0):
                        need[key] = val
                for key, val in need.items():
                    if waited.get(key, 0) >= val:
                        continue
                    waited[key] = val
                    sem = esem[key[1]] if key[0] == "c" else self.dma_sems[key[1]][0]
                    e.wait_ge(sem, val)
                ins = op.emit(e)
                if op.is_dma:
                    sem = self.dma_sems[op.dma_sem][0]
                    if not isinstance(ins, (list, tuple)):
                        ins = [ins]
                    assert len(ins) == op.ndma, (len(ins), op.ndma)
                    for i in ins:
                        i.then_inc(sem, 16)
                elif op.sig:
                    ins.then_inc(esem[engname], 1)

        @block.tensor
        def _(e):
            run("pe", e)

        @block.scalar
        def _(e):
            run("act", e)

        @block.vector
        def _(e):
            run("dve", e)

        @block.gpsimd
        def _(e):
            run("pool", e)

        @block.sync
        def _(e):
            run("sp", e)


class Arena:
    def __init__(self, ap):
        self.ap = ap

    def f32(self, off_b, *shape):
        n = 1
        for s in shape:
            n *= s
        assert off_b % 4 == 0 and off_b + 4 * n <= ARENA_KB * 1024, (off_b, shape)
        v = self.ap[:, off_b // 4: off_b // 4 + n]
        return self._shape(v, shape)

    def bf16(self, off_b, *shape):
        n = 1
        for s in shape:
            n *= s
        assert off_b % 4 == 0 and n % 2 == 0 and off_b + 2 * n <= ARENA_KB * 1024, (off_b, shape)
        v = self.ap[:, off_b // 4: off_b // 4 + n // 2].bitcast(BF16)
        return self._shape(v, shape)

    @staticmethod
    def _shape(v, shape):
        if len(shape) == 1:
            return v
        if len(shape) == 2:
            return v.rearrange("p (a b) -> p a b", a=shape[0], b=shape[1])
        if len(shape) == 3:
            return v.rearrange("p (a b c) -> p a b c", a=shape[0], b=shape[1], c=shape[2])
        raise ValueError(shape)


class Bump:
    def __init__(self, start, end):
        self.p = start
        self.end = end

    def take(self, nbytes):
        nbytes = (nbytes + 31) // 32 * 32
        o = self.p
        self.p += nbytes
        assert self.p <= self.end, ("arena region overflow", self.p, self.end)
        return o


def build(NT=16, WT=2, dbg=(), stop_after=None):
    assert NT % 4 == 0 and WT >= 1 and WT <= 4
    TT = WT + NT
    NTOK = NT * 128
    TTOK = TT * 128
    NB = NT
    NH = NT // 2
    HTOK = NH * 128

    nc = bass.Bass("TRN2", target_bir_lowering=False)

    def din(name, shape):
        return nc.dram_tensor(name, list(shape), F32, kind="ExternalInput").ap()

    xext = din("xext", [TTOK, D])
    flags = din("flags", [128, 2])
    norm_mix_g = din("norm_mix_g", [1, D])
    w_in = din("w_in", [D, INW])
    b_in = din("b_in", [1, INW])
    attn_sinks = din("attn_sinks", [1, 16])
    lb_logits = din("hgrn_lb_logits", [2, D])
    hgrn_norm_g = din("hgrn_norm_g", [1, D])
    w_ba = din("w_branch_attn", [D, D])
    w_bh = din("w_branch_hgrn", [D, D])
    w_out = din("w_out", [D, D])
    norm_ffn_g = din("norm_ffn_g", [1, D])
    w_fg = din("w_ffn_gate", [D, FF])
    w_fu = din("w_ffn_up", [D, FF])
    w_fd = din("w_ffn_down", [FF, D])
    norm_final_g = din("norm_final_g", [1, D])
    out = nc.dram_tensor("out", [NTOK, D], F32, kind="ExternalOutput").ap()
    dbg_out = {}
    for nm, shp in dbg:
        dbg_out[nm] = nc.dram_tensor("dbg_" + nm, list(shp), F32, kind="ExternalOutput").ap()

    st = ExitStack()
    with st:
        arena_t = st.enter_context(nc.sbuf_tensor("arena", [128, ARENA_KB * 256], F32))
        A = Arena(arena_t[:])
        psb = [st.enter_context(nc.psum_tensor("ps%d" % i, [128, 512], F32)) for i in range(8)]

        def psf(i, n=512):
            return psb[i][:, 0:n]

        def psbf(i):
            return psb[i][:].bitcast(BF16)

        P = Prog(nc)
        import os as _os
        if _os.environ.get('CUT'):
            P.limit = int(_os.environ['CUT'])

        def OP(eng, method, r, w, *args, **kw):
            return P.op(eng, lambda e: getattr(e, method)(*args, **kw), r=r, w=w)

        def DMA(eng, slot, r, w, out, in_, **kw):
            return P.dma(eng, lambda e: e.dma_start(out=out, in_=in_, **kw), slot, r=r, w=w)

        def ACT(r, w, **kw):
            return OP("act", "activation", r, w, **kw)

        def MM(r, w, out, lhsT, rhs, start, stop):
            return OP("pe", "matmul", r, w, out, lhsT=lhsT, rhs=rhs, start=start, stop=stop)

        def TR(r, w, out, in_):
            return OP("pe", "transpose", r, w, out, in_, ident)

        def finish():
            P.limit = None
            fin = list(P.last_dma.values())
            P.op("sp", lambda e: None, deps=fin)
            print('n_ops', len(P.ops), 'n_sems', len(P.dma_sems) + 5)
            blk = st.enter_context(nc.Block())
            P.emit_all(blk, st)
            return nc

        def bc_mid(ap2d, a, c):
            return ap2d.rearrange("p (o c) -> p o c", o=1).to_broadcast([128, a, c])

        def bc_last(ap2d, a, c):
            return ap2d.rearrange("p (a o) -> p a o", o=1).to_broadcast([128, a, c])

        def t3(ap2d, c):
            return ap2d.rearrange("p (a c) -> p a c", c=c)

        cst = Bump(0, 20 * 1024)
        R_UT = 20 * 1024
        R_YH = R_UT + TTOK * KC * 2
        R_YA = R_YH + NTOK * KC * 2
        R_END = ARENA_KB * 1024
        R_TMP = R_END - 48 * 1024
        R_MT = R_TMP - NTOK * KC * 2
        assert R_YA + NTOK * KC * 2 <= R_MT
        R_H1 = R_UT
        R_U2 = R_H1 + NH * 4096
        R_ZT = R_U2 + HTOK * KC * 2
        R_GAP = R_ZT + FC * HTOK * 2
        assert R_GAP + 8192 <= R_MT, (R_GAP, R_MT)

        uT = A.bf16(R_UT, KC, TTOK)
        YH = A.bf16(R_YH, KC, NTOK)
        YA = A.bf16(R_YA, KC, NTOK)
        MT = A.bf16(R_MT, KC, NTOK)

        ident = A.bf16(cst.take(256), 128)
        cmask = A.bf16(cst.take(256), 128)
        maskc = A.bf16(cst.take(256), 128)
        maskp = A.bf16(cst.take(256), 128)
        maskp0 = A.bf16(cst.take(256), 128)
        onesev = A.bf16(cst.take(256), 128)
        onesod = A.bf16(cst.take(256), 128)
        RM = A.f32(cst.take(2048), 512)
        bT = A.f32(cst.take(58 * 4), 58)
        bkd = A.f32(cst.take(8), 2)
        BH = A.f32(cst.take(4096), 1024)
        BV2 = A.f32(cst.take(1024), 2, 128)
        gT1 = A.f32(cst.take(32), KC)
        gT2 = A.f32(cst.take(32), KC)
        ngT = A.f32(cst.take(32), 8)
        lgt = A.f32(cst.take(64), 2, 8)
        lbT = A.f32(cst.take(32), 8)
        omlT = A.f32(cst.take(32), 8)
        lsum = A.f32(cst.take(32), 8)
        c0T = A.f32(cst.take(32), 8)
        c1T = A.f32(cst.take(32), 8)
        bTh = A.f32(cst.take(58 * 4), 58)
        snk = A.f32(cst.take(64), 16)
        flg = A.f32(cst.take(8), 2)
        nhalf = A.f32(cst.take(64), 16)
        ss0 = A.f32(cst.take(4 * TT), TT)
        ms0 = A.f32(cst.take(4 * TT), TT)
        rs0 = A.f32(cst.take(4 * TT), TT)
        ss1 = A.f32(cst.take(4 * NT), NT)
        ms1 = A.f32(cst.take(4 * NT), NT)
        rs1 = A.f32(cst.take(4 * NT), NT)
        ss2 = A.f32(cst.take(4 * NT), NT)
        ms2 = A.f32(cst.take(4 * NT), NT)
        rs2 = A.f32(cst.take(4 * NT), NT)

        with nc.allow_non_contiguous_dma(reason="small one-time constant loads"):
            DMA("sp", "c_bT", [], ["bT"], bT, b_in.rearrange("o (j p) -> p (o j)", p=128), allow_slow_non_contiguous=True)
            DMA("sp", "c_g1", [], ["gT1"], gT1, norm_mix_g.rearrange("o (k p) -> p (o k)", p=128), allow_slow_non_contiguous=True)
            DMA("sp", "c_g2", [], ["gT2"], gT2, norm_ffn_g.rearrange("o (k p) -> p (o k)", p=128), allow_slow_non_contiguous=True)
            DMA("sp", "c_ng", [], ["ngT"], ngT, hgrn_norm_g.rearrange("o (h p) -> p (o h)", p=128), allow_slow_non_contiguous=True)
            DMA("sp", "c_lg", [], ["lgt"], lgt, lb_logits.rearrange("l (h p) -> p l h", p=128), allow_slow_non_contiguous=True)
            for g in range(2):
                for half in range(2):
                    DMA("sp", "c_bkd%d%d" % (g, half), [], [("bkd", g, half)], bkd[half * 64:(half + 1) * 64, g:g + 1],
                        b_in[0:1, C_AK + g * 64: C_AK + (g + 1) * 64].rearrange("o c -> c o"), allow_slow_non_contiguous=True)
        bkd_keys = [("bkd", g, half) for g in range(2) for half in range(2)]
        DMA("sp", "c_BH", [], ["BH"], BH, b_in[0:1, C_HI:C_HI + 1024].partition_broadcast(128))
        for g in range(2):
            for half in range(2):
                DMA("sp", "c_bv2%d%d" % (g, half), [], [("BV2", g, half)], BV2[:, g, half * 64:(half + 1) * 64],
                    b_in[0:1, C_AV + g * 64: C_AV + (g + 1) * 64].partition_broadcast(128))
        DMA("sp", "c_snk", [], ["snk"], snk, attn_sinks[0:1, :].partition_broadcast(128))
        DMA("sp", "c_flg", [], ["flg"], flg, flags)

        OP("pool", "memset", [], ["ident"], ident, 1.0)
        OP("pool", "affine_select", ["ident"], ["ident"], out=ident, in_=ident, pattern=[[-1, 128]], compare_op=ALU.is_equal,
           fill=0.0, base=0, channel_multiplier=1)
        OP("pool", "memset", [], ["maskc"], maskc, 1.0)
        OP("pool", "affine_select", ["maskc"], ["maskc"], out=maskc, in_=maskc, pattern=[[1, 128]], compare_op=ALU.is_ge,
           fill=0.0, base=0, channel_multiplier=-1)
        OP("pool", "memset", [], ["maskp"], maskp, 1.0)
        OP("pool", "affine_select", ["maskp"], ["maskp"], out=maskp, in_=maskp, pattern=[[-1, 128]], compare_op=ALU.is_gt,
           fill=0.0, base=0, channel_multiplier=1)
        OP("dve", "tensor_scalar", ["maskp", "flg"], ["maskp0"], out=maskp0, in0=maskp, scalar1=flg[:, 0:1], scalar2=None, op0=ALU.mult)
        OP("dve", "tensor_copy", ["maskc"], ["cmask"], out=cmask, in_=maskc)
        OP("dve", "memset", ["cmask"], ["cmask"], cmask[0:64, 64:128], 0.0)
        OP("dve", "memset", [], ["onesev"], onesev, 0.0)
        OP("dve", "memset", ["onesev"], ["onesev"], onesev[:, 0:64], 1.0)
        OP("dve", "memset", [], ["onesod"], onesod, 0.0)
        OP("dve", "memset", ["onesod"], ["onesod"], onesod[:, 64:128], 1.0)
        OP("dve", "memset", [], ["RM"], RM, 1.0)
        OP("dve", "memset", ["RM"], ["RM"], t3(RM, 64)[:, :, 0:1], 0.0)
        OP("dve", "memset", [], ["nhalf"], nhalf, -0.5)
        ACT(["lgt"], ["lgt"], out=lgt, in_=lgt, func=AF.Exp)
        OP("dve", "tensor_tensor", ["lgt"], ["lsum"], out=lsum, in0=lgt[:, 0, :], in1=lgt[:, 1, :], op=ALU.add)
        OP("dve", "reciprocal", ["lsum"], ["lsum"], out=lsum, in_=lsum)
        OP("dve", "tensor_tensor", ["lgt", "lsum"], ["lbT"], out=lbT, in0=lgt[:, 0, :], in1=lsum, op=ALU.mult)
        OP("dve", "tensor_scalar", ["lbT"], ["omlT"], out=omlT, in0=lbT, scalar1=-1.0, scalar2=1.0, op0=ALU.mult, op1=ALU.add)
        OP("dve", "tensor_scalar", ["omlT"], ["c1T"], out=c1T, in0=omlT, scalar1=0.5, scalar2=None, op0=ALU.mult)
        OP("dve", "tensor_tensor", ["lbT", "c1T"], ["c0T"], out=c0T, in0=lbT, in1=c1T, op=ALU.add)
        OP("dve", "tensor_scalar", ["bT"], ["bTh"], out=bTh, in0=bT, scalar1=0.5, scalar2=None, op0=ALU.mult)
        ACT(["snk"], ["snk"], out=snk, in_=snk, func=AF.Exp)

        if stop_after == "K":
            return finish()

        def wload(dst, src2d, c0, ncols, slot, key, r=()):
            srcv = src2d.rearrange("(k p) c -> p k c", p=128)[:, :, c0:c0 + ncols]
            return DMA("pool", slot, list(r), [key], dst, srcv)

        def uTk(t0, nt):
            return [("uT", t) for t in range(t0, t0 + nt)]

        def proj_fm(ps_ap, wt, wkeys, tok0, ntok, pskey):
            for k in range(KC):
                MM(list(wkeys) + uTk(tok0 // 128, ntok // 128), [pskey], ps_ap, wt[:, k, :], uT[:, k, tok0:tok0 + ntok],
                   k == 0, k == KC - 1)

        def dump(name, src3, nk, ncol, rkeys, tmp_off):
            if name not in dbg_out:
                return
            P.barrier()
            dtmp = [A.f32(tmp_off, ncol), A.f32(tmp_off + 4 * ncol, ncol)]
            for k in range(nk):
                OP("dve", "tensor_copy", list(rkeys), ["dtmp%d" % (k % 2)], out=dtmp[k % 2], in_=src3[:, k, :])
                DMA("sp", "dbg_%s_%d" % (name, k % 2), ["dtmp%d" % (k % 2)], [], dbg_out[name][k * 128:(k + 1) * 128, :], dtmp[k % 2])
            P.barrier()

        overlap0 = NTOK * KC >= 16 * 1024
        tmp0 = Bump(R_YH + NTOK * KC, R_YH + NTOK * KC * 2) if overlap0 else Bump(R_TMP, R_END)
        xs = [A.f32(tmp0.take(4096), 1024) for _ in range(3)]
        un = [A.bf16(tmp0.take(2048), 1024) for _ in range(2)]
        for t in range(TT):
            xb = xs[t % 3]
            ub = un[t % 2]
            kx, ku = ("xs", t % 3), ("un", t % 2)
            DMA("sp", "xs%d" % (t % 3), [], [kx], xb, xext[t * 128:(t + 1) * 128, :])
            ACT([kx], [("ss0", t), ku], out=ub, in_=xb, func=AF.Square, accum_out=ss0[:, t:t + 1])
            OP("dve", "tensor_scalar", [("ss0", t)], [("ms0", t)], out=ms0[:, t:t + 1], in0=ss0[:, t:t + 1], scalar1=1.0 / D,
               scalar2=EPS, op0=ALU.mult, op1=ALU.add)
            OP("pool", "tensor_tensor", [("ms0", t), "nhalf"], [("rs0", t)], out=rs0[:, t:t + 1], in0=ms0[:, t:t + 1],
               in1=nhalf[:, 0:1], op=ALU.pow)
            OP("dve", "tensor_scalar", [kx, ("rs0", t)], [ku], out=ub, in0=xb, scalar1=rs0[:, t:t + 1], scalar2=None, op0=ALU.mult)
            pst = t3(psbf(t % 2), 128)
            for k in range(KC):
                TR([ku, "ident"], [("psJ", t % 2)], pst[:, k, :], ub[:, k * 128:(k + 1) * 128])
            OP("dve", "tensor_tensor", [("psJ", t % 2), "gT1"], [("uT", t)], out=uT[:, :, t * 128:(t + 1) * 128], in0=pst,
               in1=bc_last(gT1, KC, 128), op=ALU.mult)
        dump("uT", uT, KC, TTOK, uTk(0, TT), R_YH)
        if stop_after == "0":
            return finish()

        if not overlap0:
            P.barrier()
        R_A = R_END - 115 * 1024
        assert R_A >= R_YA
        ta = Bump(R_A, R_END)
        HQ = 4
        WHQ = [A.bf16(ta.take(KC * 512 * 2), KC, 512) for _ in range(HQ)]
        S_f = [[A.f32(ta.take(512), 128) for _ in range(2)] for _ in range(HQ)]
        SSQ = A.f32(ta.take(64), HQ, 4)
        MSQ = A.f32(ta.take(64), HQ, 4)
        RSQ = A.f32(ta.take(64), HQ, 4)
        S_b = [[A.bf16(ta.take(256), 128) for _ in range(4)] for _ in range(HQ)]
        vflag = flg[:, 0:1]

        class TB:
            pass
        TS = []
        for _ in range(2):
            T_ = TB()
            for nm in ("F", "G", "KK", "B", "X1n", "XB", "Q"):
                setattr(T_, nm, A.f32(ta.take(2048), 512))
            T_.E2 = T_.G
            T_.E1 = T_.F
            TS.append(T_)
        HB = []
        for hq in range(HQ):
            b = TB()
            for nm in ("KLT", "QD", "KD", "QE", "GT"):
                setattr(b, nm, A.bf16(ta.take(1024), 512))
            b.KLa = A.bf16(ta.take(1024), 4, 128)
            b.KLb = A.bf16(ta.take(1024), 4, 128)
            b.V = A.bf16(ta.take(1024), 4, 128)
            b.ORAW = A.f32(ta.take(2048), 4, 128)
            b.ON = A.bf16(ta.take(1024), 4, 128)
            b.DEC = A.f32(ta.take(32), 8)
            b.Am = [A.bf16(ta.take(256), 128) for _ in range(2)]
            HB.append(b)
            OP("dve", "memset", [], [("KLa", hq), ("KLb", hq)], b.KLa, 0.0)
            OP("dve", "memset", [("KLb", hq)], [("KLb", hq)], b.KLb, 0.0)

        groups = []
        t = 0
        while t < WT:
            n = min(4, WT - t)
            groups.append((t, n, True))
            t += n
        for gi in range(NT // 4):
            groups.append((WT + 4 * gi, 4, False))

        PS_V, PS_T = 2, 3
        psT = psbf(PS_T)
        pjc = [0]

        PJ_BANKS = [0, 1, 4, 5, 6, 7]

        def pjbank():
            bnk = PJ_BANKS[pjc[0] % len(PJ_BANKS)]
            pjc[0] += 1
            return bnk, (("psJ", bnk) if bnk < 4 else ("psR", bnk - 4))

        def load_head_w(h):
            hq = h % HQ
            for ci, c0 in enumerate((C_HQ, C_HF, C_HI, C_HG)):
                wload(WHQ[hq][:, :, ci * 128:(ci + 1) * 128], w_in, c0 + h * 128, 128, "whq%d_%d" % (hq, ci), ("WHQ", hq, ci))

        pjsel = {}

        def S1a_pe(h, t0, nt, is_pre):
            hq = h % HQ
            b = HB[hq]
            wh = WHQ[hq]
            n = nt * 128
            tok0 = t0 * 128

            def K(nm):
                return (nm, hq)
            bk, bkey = pjbank()
            pjsel[(h, "F")] = (bk, bkey)
            proj_fm(psf(bk, n), wh[:, :, 128:256], [("WHQ", hq, 1)], tok0, n, bkey)
            for tl in range(nt):
                for k in range(KC):
                    MM([("WHQ", hq, 2), ("uT", t0 + tl)], ["psV"], psf(PS_V)[:, tl * 128:(tl + 1) * 128],
                       uT[:, k, (t0 + tl) * 128:(t0 + tl + 1) * 128], wh[:, k, 256:384], k == 0, k == KC - 1)
            for tl in range(nt):
                OP("dve", "tensor_tensor", ["psV", "BH"], [K(("V", tl))], out=b.V[:, tl, :], in0=psf(PS_V)[:, tl * 128:(tl + 1) * 128],
                   in1=BH[:, h * 128:(h + 1) * 128], op=ALU.add)
            if is_pre:
                return
            bk, bkey = pjbank()
            pjsel[(h, "Q")] = (bk, bkey)
            proj_fm(psf(bk, n), wh[:, :, 0:128], [("WHQ", hq, 0)], tok0, n, bkey)
            bk, bkey = pjbank()
            pjsel[(h, "G")] = (bk, bkey)
            proj_fm(psf(bk, n), wh[:, :, 384:512], [("WHQ", hq, 3)], tok0, n, bkey)

        def S1a_ev(h, t0, nt, is_pre):
            hq = h % HQ
            b = HB[hq]
            T = TS[h % 2]
            n = nt * 128

            def K(nm):
                return (nm, hq)

            def TK_(nm):
                return ("T", h % 2, nm)
            bk, bkey = pjsel[(h, "F")]
            cF = C_HF // 128 + h
            ACT([bkey, "bTh"], [TK_("F")], out=T.F[:, 0:n], in_=psf(bk, n), func=AF.Tanh, bias=bTh[:, cF:cF + 1], scale=0.5)
            if is_pre:
                return
            bk, bkey = pjsel[(h, "Q")]
            cQ = C_HQ // 128 + h
            ACT([bkey, "bTh"], [TK_("Q")], out=T.Q[:, 0:n], in_=psf(bk, n), func=AF.Tanh, bias=bTh[:, cQ:cQ + 1], scale=0.5)
            OP("dve", "tensor_scalar", [bkey, "bTh", TK_("Q")], [TK_("XB")], out=T.XB[:, 0:n], in0=psf(bk, n), scalar1=0.5,
               scalar2=bTh[:, cQ:cQ + 1], op0=ALU.mult, op1=ALU.add)
            OP("dve", "scalar_tensor_tensor", [TK_("Q"), TK_("XB")], [TK_("Q")], out=T.Q[:, 0:n], in0=T.Q[:, 0:n], scalar=1.0,
               in1=T.XB[:, 0:n], op0=ALU.add, op1=ALU.mult)
            bk, bkey = pjsel[(h, "G")]
            cG = C_HG // 128 + h
            ACT([bkey, "bTh"], [K("GT")], out=b.GT[:, 0:n], in_=psf(bk, n), func=AF.Tanh, bias=bTh[:, cG:cG + 1], scale=0.5)
            OP("pool", "tensor_scalar", [K("GT")], [K("GT")], out=b.GT[:, 0:n], in0=b.GT[:, 0:n], scalar1=0.5, scalar2=0.5,
               op0=ALU.mult, op1=ALU.add)

        def S1b(h, t0, nt, is_pre, preload):
            hq = h % HQ
            b = HB[hq]
            T = TS[h % 2]
            n = nt * 128
            nch = nt * 2

            def K(nm):
                return (nm, hq)

            def TK_(nm):
                return ("T", h % 2, nm)
            B3 = t3(T.B[:, 0:n], 64)
            OP("dve", "tensor_scalar", [TK_("F"), "c1T", "c0T"], [TK_("F")], out=T.F[:, 0:n], in0=T.F[:, 0:n],
               scalar1=c1T[:, h:h + 1], scalar2=c0T[:, h:h + 1], op0=ALU.mult, op1=ALU.add)
            ACT([TK_("F")], [TK_("G")], out=T.G[:, 0:n], in_=T.F[:, 0:n], func=AF.Ln)
            OP("pool", "tensor_scalar", [TK_("F")], [TK_("KK")], out=T.KK[:, 0:n], in0=T.F[:, 0:n], scalar1=-1.0, scalar2=1.0,
               op0=ALU.mult, op1=ALU.add)

        def S1b2(h, t0, nt, is_pre):
            hq = h % HQ
            b = HB[hq]
            T = TS[h % 2]
            n = nt * 128
            nch = nt * 2

            def K(nm):
                return (nm, hq)

            def TK_(nm):
                return ("T", h % 2, nm)
            B3 = t3(T.B[:, 0:n], 64)
            OP("dve", "tensor_tensor_scan", [TK_("G"), "RM"], [TK_("B")], out=T.B[:, 0:n], data0=RM[:, 0:n], data1=T.G[:, 0:n],
               initial=0.0, op0=ALU.mult, op1=ALU.add)
            OP("pool", "tensor_tensor", [TK_("B")], [TK_("G")], out=t3(T.E2[:, 0:n], 64),
               in0=B3[:, :, 63:64].to_broadcast([128, nch, 64]), in1=B3, op=ALU.subtract)
            if not is_pre:
                OP("pool", "tensor_tensor", [TK_("B")], [TK_("F")], out=t3(T.E1[:, 0:n], 64), in0=B3,
                   in1=B3[:, :, 31:32].to_broadcast([128, nch, 64]), op=ALU.subtract)
            ACT([TK_("G")], [TK_("G")], out=T.E2[:, 0:n], in_=T.E2[:, 0:n], func=AF.Exp)
            ACT([TK_("B")], [K("DEC")], out=b.DEC[:, 0:nch].rearrange("p (c o) -> p c o", o=1), in_=B3[:, :, 63:64], func=AF.Exp)
            if not is_pre:
                ACT([TK_("F")], [TK_("X1n")], out=T.X1n[:, 0:n], in_=T.E1[:, 0:n], func=AF.Exp, scale=-1.0)
                ACT([TK_("F")], [TK_("F")], out=T.E1[:, 0:n], in_=T.E1[:, 0:n], func=AF.Exp)
                ACT([TK_("B")], [TK_("XB")], out=T.XB[:, 0:n], in_=T.B[:, 0:n], func=AF.Exp)
            OP("dve", "tensor_tensor", [TK_("KK"), TK_("G")], [K("KLT")], out=b.KLT[:, 0:n], in0=T.KK[:, 0:n], in1=T.E2[:, 0:n], op=ALU.mult)
            if not is_pre:
                OP("dve", "tensor_tensor", [TK_("Q"), TK_("F")], [K("QD")], out=b.QD[:, 0:n], in0=T.Q[:, 0:n], in1=T.E1[:, 0:n], op=ALU.mult)
                OP("dve", "tensor_tensor", [TK_("KK"), TK_("X1n")], [K("KD")], out=b.KD[:, 0:n], in0=T.KK[:, 0:n], in1=T.X1n[:, 0:n], op=ALU.mult)
                OP("dve", "tensor_tensor", [TK_("Q"), TK_("XB")], [K("QE")], out=b.QE[:, 0:n], in0=T.Q[:, 0:n], in1=T.XB[:, 0:n], op=ALU.mult)
            for tl in range(nt):
                TR([K("KLT"), "ident"], ["psT"], psT[:, tl * 128:(tl + 1) * 128], b.KLT[:, tl * 128:(tl + 1) * 128])
            kt3 = t3(psT[:, 0:nt * 128], 128)
            ACT(["psT"], [K("KLa")], out=b.KLa[0:64, 0:nt, :], in_=kt3[0:64], func=AF.Copy)
            ACT(["psT"], [K("KLb")], out=b.KLb[64:128, 0:nt, :], in_=kt3[64:128], func=AF.Copy)

        chunk_ctr = [0] * HQ

        def S2_tile(heads, t0, tl, is_pre):
            par = tl % 2
            sl = slice(tl * 128, (tl + 1) * 128)
            info = []
            for h in heads:
                hq = h % HQ
                b = HB[hq]
                ps_r = psf(4 + hq)
                c0i = chunk_ctr[hq]
                info.append(dict(
                    h=h, hq=hq, b=b, AT=ps_r[:, 0:128], P0=ps_r[:, 128:256], P1=ps_r[:, 256:384], O=ps_r[:, 384:512], rk=("psR", hq),
                    sb0=S_b[hq][c0i % 4], sb1=S_b[hq][(c0i + 1) % 4], sb2=S_b[hq][(c0i + 2) % 4],
                    kb0=("S_b", hq, c0i % 4), kb1=("S_b", hq, (c0i + 1) % 4), kb2=("S_b", hq, (c0i + 2) % 4),
                    sA=S_f[hq][0], sB=S_f[hq][1], kA=("S_f", hq, 0), kB=("S_f", hq, 1)))

            def K(d, nm):
                return (nm, d["hq"])
            for d in info:
                b = d["b"]
                if not is_pre:
                    MM([K(d, "KD"), K(d, "QD")], [d["rk"]], d["AT"], b.KD[:, sl], b.QD[:, sl], True, True)
                MM([K(d, "KLa"), K(d, ("V", tl))], [d["rk"]], d["P0"], b.KLa[:, tl, :], b.V[:, tl, :], True, True)
                MM([K(d, "KLb"), K(d, ("V", tl))], [d["rk"]], d["P1"], b.KLb[:, tl, :], b.V[:, tl, :], True, True)
            for d in info:
                b = d["b"]
                OP("dve", "scalar_tensor_tensor", [d["kA"], K(d, "DEC"), d["rk"]], [d["kB"]], out=d["sB"], in0=d["sA"],
                   scalar=b.DEC[:, 2 * tl:2 * tl + 1], in1=d["P0"], op0=ALU.mult, op1=ALU.add)
            if not is_pre:
                for d in info:
                    ACT([d["kB"]], [d["kb1"]], out=d["sb1"], in_=d["sB"], func=AF.Copy)
            for d in info:
                b = d["b"]
                OP("dve", "scalar_tensor_tensor", [d["kB"], K(d, "DEC"), d["rk"]], [d["kA"]], out=d["sA"], in0=d["sB"],
                   scalar=b.DEC[:, 2 * tl + 1:2 * tl + 2], in1=d["P1"], op0=ALU.mult, op1=ALU.add)
                if is_pre and (t0 + tl == WT - 1):
                    OP("dve", "tensor_scalar", [d["kA"], "flg"], [d["kA"]], out=d["sA"], in0=d["sA"], scalar1=vflag, scalar2=None, op0=ALU.mult)
            if not is_pre:
                for d in info:
                    OP("dve", "tensor_tensor", [d["rk"], "cmask"], [K(d, ("Am", par))], out=d["b"].Am[par], in0=d["AT"], in1=cmask, op=ALU.mult)
            for d in info:
                ACT([d["kA"]], [d["kb2"]], out=d["sb2"], in_=d["sA"], func=AF.Copy)
            if not is_pre:
                for d in info:
                    b = d["b"]
                    O = d["O"]
                    MM([K(d, ("Am", par)), K(d, ("V", tl))], [d["rk"]], O, b.Am[par], b.V[:, tl, :], True, False)
                    MM([K(d, "QE"), d["kb0"]], [d["rk"]], O[0:64, :], b.QE[:, tl * 128:tl * 128 + 64], d["sb0"], False, True)
                    MM([K(d, "QE"), d["kb1"]], [d["rk"]], O[64:128, :], b.QE[:, tl * 128 + 64:tl * 128 + 128], d["sb1"], False, True)
                for d in info:
                    b = d["b"]
                    ACT([d["rk"]], [("SSQ", d["hq"], tl), K(d, ("ORAW", tl))], out=b.ORAW[:, tl, :], in_=d["O"], func=AF.Square,
                        accum_out=SSQ[:, d["hq"], tl:tl + 1])
                for d in info:
                    b = d["b"]
                    OP("dve", "tensor_copy", [d["rk"]], [K(d, ("ORAW", tl))], out=b.ORAW[:, tl, :], in_=d["O"])
            for h in heads:
                chunk_ctr[h % HQ] += 2

        def S3_quad(heads, t0, nt):
            n = nt * 128
            nh = len(heads)
            sskeys = [("SSQ", h % HQ, tl) for h in heads for tl in range(nt)]
            OP("dve", "tensor_scalar", sskeys, ["MSQ"], out=MSQ[:, :, 0:nt], in0=SSQ[:, :, 0:nt],
               scalar1=1.0 / 128.0, scalar2=EPS, op0=ALU.mult, op1=ALU.add)
            OP("pool", "tensor_tensor", ["MSQ", "nhalf"], ["RSQ"], out=RSQ[:, :, 0:nt], in0=MSQ[:, :, 0:nt],
               in1=bc_last(nhalf[:, 0:HQ], HQ, nt), op=ALU.pow)
            for h in heads:
                hq = h % HQ
                b = HB[hq]
                OP("dve", "tensor_tensor", [(("ORAW", tl), hq) for tl in range(nt)] + ["RSQ"], [("ON", hq)], out=b.ON[:, 0:nt, :],
                   in0=b.ORAW[:, 0:nt, :], in1=bc_last(RSQ[:, hq, 0:nt], nt, 128), op=ALU.mult)
            mt0 = (t0 - WT) * 128
            for h in heads:
                hq = h % HQ
                b = HB[hq]
                if hq % 2 == 0:
                    dst, dkey = psT[:, 512:1024], "psT"
                else:
                    dst, dkey = psbf(PS_V)[:, 0:512], "psV"
                for tl in range(nt):
                    TR([("ON", hq), "ident"], [dkey], dst[:, tl * 128:(tl + 1) * 128], b.ON[:, tl, :])
                OP("dve", "scalar_tensor_tensor", [dkey, "ngT", ("GT", hq)], [("YH", h, mt0 // 512)], out=YH[:, h, mt0:mt0 + n],
                   in0=dst[:, 0:n], scalar=ngT[:, h:h + 1], in1=b.GT[:, 0:n], op0=ALU.mult, op1=ALU.add if False else ALU.mult)

        for q in range(8 // HQ):
            heads = list(range(q * HQ, (q + 1) * HQ))
            if q == 0:
                for h in heads:
                    load_head_w(h)
            for h in heads:
                hq = h % HQ
                OP("dve", "memset", [], [("S_f", hq, 0)], S_f[hq][0], 0.0)
                OP("dve", "memset", [], [("S_b", hq, 0)], S_b[hq][0], 0.0)
                chunk_ctr[hq] = 0
            for gi_, (t0, nt, is_pre) in enumerate(groups):
                pairs = [heads[i_:i_ + 2] for i_ in range(0, len(heads), 2)]
                last_g = (gi_ == len(groups) - 1 and q + 1 < 8 // HQ)

                def pe_stage(pair):
                    for h in pair:
                        S1a_pe(h, t0, nt, is_pre)
                        if last_g:
                            load_head_w(h + HQ)

                pe_stage(pairs[0])
                for pi_, pair in enumerate(pairs):
                    for h in pair:
                        S1a_ev(h, t0, nt, is_pre)
                    for h in pair:
                        S1b(h, t0, nt, is_pre, False)
                    if pi_ + 1 < len(pairs):
                        pe_stage(pairs[pi_ + 1])
                    for h in pair:
                        S1b2(h, t0, nt, is_pre)
                for tl in range(nt):
                    S2_tile(heads, t0, tl, is_pre)
                if not is_pre:
                    S3_quad(heads, t0, nt)

        YH_keys = [("YH", h, g) for h in range(8) for g in range(NT // 4)]
        dump("YH", YH, KC, NTOK, YH_keys, R_A)

        if stop_after == "A":
            return finish()

        P.barrier()
        R_AB = R_END - 80 * 1024
        assert R_YA + NTOK * KC * 2 <= R_AB
        tb_ = Bump(R_AB, R_END)
        WQ = [A.bf16(tb_.take(KC * 512 * 2), KC, 512) for _ in range(2)]
        WKV = [A.bf16(tb_.take(KC * 256 * 2), KC, 256) for _ in range(2)]
        QT = A.bf16(tb_.take(NB * 512 * 2), NB, 4, 128)
        KT = A.bf16(tb_.take((NB + 1) * 128 * 2), (NB + 1) * 128)
        Vev = A.bf16(tb_.take((NB + 1) * 128 * 2), NB + 1, 128)
        Vod = A.bf16(tb_.take((NB + 1) * 128 * 2), NB + 1, 128)
        PT = [[[A.bf16(tb_.take(1024), 512) for par in range(2)] for kb in range(2)] for buf in range(2)]
        ZS = [A.f32(tb_.take(2048), 512) for _ in range(2)]
        ESK = A.f32(tb_.take(4096), 2, 512)
        MK = {nm: A.bf16(tb_.take(1024), 4, 128) for nm in ("maskc", "maskp", "maskp0")}
        BV2R = A.f32(tb_.take(4096), 2, 2, 4, 64) if False else None
        BVR = [[A.f32(tb_.take(1024), 4, 64) for half in range(2)] for g in range(2)]
        for g in range(2):
            for par in range(2):
                for pr in range(4):
                    hd = g * 8 + 2 * pr + par
                    OP("dve", "tensor_copy", ["snk"], [("ESK", g, par, pr)],
                       out=ESK[par * 64:(par + 1) * 64, g, pr * 128:(pr + 1) * 128],
                       in_=snk[par * 64:(par + 1) * 64, hd:hd + 1].to_broadcast([64, 128]))
        ESK_keys = [[("ESK", g, par, pr) for par in range(2) for pr in range(4)] for g in range(2)]
        for nm, src in (("maskc", maskc), ("maskp", maskp), ("maskp0", maskp0)):
            for a_ in range(4):
                OP("dve", "tensor_copy", [nm], [("MK", nm, a_)], out=MK[nm][:, a_, :], in_=src)
        MK_keys = {nm: [("MK", nm, a_) for a_ in range(4)] for nm in MK}
        for g in range(2):
            for half in range(2):
                for a_ in range(4):
                    OP("dve", "tensor_copy", [("BV2", g, half)], [("BVR", g, half, a_)], out=BVR[g][half][:, a_, :],
                       in_=BV2[:, g, half * 64:(half + 1) * 64])
        OP("dve", "memset", [], ["Vev"], Vev, 0.0)
        OP("dve", "memset", [], ["Vod"], Vod, 0.0)
        KTOK0 = (WT - 1) * 128
        NKT = (NB + 1) * 128
        pj = 0
        for g in range(2):
            wq, wkv = WQ[g % 2], WKV[g % 2]
            wload(wq, w_in, C_AQ + g * 512, 512, "wq%d" % g, ("WQ", g))
            for i, c0 in enumerate((C_AK + g * 64, C_AK + g * 64, C_AV + g * 64, C_AV + g * 64)):
                wload(wkv[:, :, i * 64:(i + 1) * 64], w_in, c0, 64, "wkv%d_%d" % (g, i), ("WKV", g, i))
            for c in range(0, NKT, 512):
                n = min(512, NKT - c)
                bank = 6 + pj % 2
                pj += 1
                proj_fm(psf(bank, n), wkv[:, :, 0:128], [("WKV", g, 0), ("WKV", g, 1)], KTOK0 + c, n, ("psP", bank))
                ACT([("psP", bank)] + bkd_keys, [("KT", c // 512)], out=KT[:, c:c + n], in_=psf(bank, n), func=AF.Identity,
                    bias=bkd[:, g:g + 1], scale=1.0)
            KT_keys = [("KT", c // 512) for c in range(0, NKT, 512)]
            for j0 in range(0, NB + 1, 4):
                nb_ = min(4, NB + 1 - j0)
                bank = 6 + pj % 2
                pj += 1
                for jj in range(nb_):
                    te = WT - 1 + j0 + jj
                    for k in range(KC):
                        MM([("WKV", g, 2), ("WKV", g, 3), ("uT", te)], [("psP", bank)], psf(bank)[:, jj * 128:(jj + 1) * 128],
                           uT[:, k, te * 128:(te + 1) * 128], wkv[:, k, 128:256], k == 0, k == KC - 1)
                pv3 = t3(psf(bank, nb_ * 128), 128)
                OP("dve", "tensor_tensor", [("psP", bank), "Vev"] + [("BVR", g, 0, a_) for a_ in range(4)], [("Vev", j0 // 4)],
                   out=Vev[:, j0:j0 + nb_, 0:64], in0=pv3[:, :, 0:64], in1=BVR[g][0][:, 0:nb_, :], op=ALU.add)
                OP("dve", "tensor_tensor", [("psP", bank), "Vod"] + [("BVR", g, 1, a_) for a_ in range(4)], [("Vod", j0 // 4)],
                   out=Vod[:, j0:j0 + nb_, 64:128], in0=pv3[:, :, 64:128], in1=BVR[g][1][:, 0:nb_, :], op=ALU.add)
            V_keys = [("Vev", j // 4) for j in range(0, NB + 1, 4)] + [("Vod", j // 4) for j in range(0, NB + 1, 4)]
            for j in range(4):
                for tg in range(NT // 4):
                    bank = 6 + pj % 2
                    pj += 1
                    proj_fm(psf(bank), wq[:, :, j * 128:(j + 1) * 128], [("WQ", g)], (WT + 4 * tg) * 128, 512, ("psP", bank))
                    cq = C_AQ // 128 + g * 4 + j
                    ACT([("psP", bank), "bT"], [("QT", tg, j)], out=QT[:, 4 * tg:4 * tg + 4, j, :], in_=t3(psf(bank), 128),
                        func=AF.Identity, bias=bT[:, cq:cq + 1], scale=1.0)
            def stage1(nblk, g=g):
                buf = nblk % 2
                qkeys = [("QT", nblk // 4, j) for j in range(4)]
                for kb in range(2):
                    kblk = nblk + kb
                    mkey = "maskc" if kb == 1 else ("maskp0" if nblk == 0 else "maskp")
                    for par in range(2):
                        bank = kb * 2 + par
                        pt = PT[buf][kb][par]
                        ptk = ("PT", buf, kb, par)
                        MM(KT_keys + qkeys, [("psS", bank)], psf(bank), KT[par * 64:(par + 1) * 64, kblk * 128:(kblk + 1) * 128],
                           QT[par * 64:(par + 1) * 64, nblk, :, :].rearrange("p a c -> p (a c)"), True, True)
                        ACT([("psS", bank)], [ptk], out=pt, in_=psf(bank), func=AF.Exp, scale=0.125)
                        OP("pool", "tensor_tensor", [ptk] + MK_keys[mkey], [ptk], out=t3(pt, 128), in0=t3(pt, 128), in1=MK[mkey], op=ALU.mult)

            def stage2(nblk, g=g):
                buf = nblk % 2
                by, bz = (4, 5) if buf == 0 else (6, 7)
                ky, kz = (("psY" if buf == 0 else ("psP", 6))), (("psZ" if buf == 0 else ("psP", 7)))
                first = True
                for kb in range(2):
                    for par in range(2):
                        vt = (Vev if par == 0 else Vod)[:, nblk + kb, :]
                        last = (kb == 1 and par == 1)
                        MM(V_keys + [("PT", buf, kb, par)], [ky], psf(by), vt, PT[buf][kb][par], first, last)
                        MM(["onesev", "onesod", ("PT", buf, kb, par)], [kz], psf(bz), onesev if par == 0 else onesod, PT[buf][kb][par], first, last)
                        first = False
                zs = ZS[buf]
                OP("dve", "tensor_tensor", [kz] + ESK_keys[g], [("ZS", buf)], out=zs, in0=psf(bz), in1=ESK[:, g, :], op=ALU.add)
                OP("dve", "reciprocal", [("ZS", buf)], [("ZS", buf)], out=zs, in_=zs)
                OP("dve", "tensor_tensor", [ky, ("ZS", buf)], [("YA", g, nblk)], out=YA[:, g * 4:(g + 1) * 4, nblk * 128:(nblk + 1) * 128],
                   in0=t3(psf(by), 128), in1=t3(zs, 128), op=ALU.mult)

            stage1(0)
            for nblk in range(NB):
                if nblk + 1 < NB:
                    stage1(nblk + 1)
                stage2(nblk)
        YA_keys = [("YA", g, n_) for g in range(2) for n_ in range(NB)]
        dump("YA", YA, KC, NTOK, YA_keys, R_AB)
        if stop_after == "B":
            return finish()

        P.barrier()
        tc_ = Bump(R_TMP, R_END)
        WC = [A.bf16(tc_.take(KC * 512 * 2), KC, 512) for _ in range(2)]
        GAt = [A.bf16(tc_.take(1024), 512) for _ in range(2)]
        GBt = [A.bf16(tc_.take(1024), 512) for _ in range(2)]
        M1 = [A.f32(tc_.take(2048), 512) for _ in range(2)]
        M2 = [A.f32(tc_.take(2048), 512) for _ in range(2)]
        it = 0
        for j in range(8):
            wc = WC[j % 2]
            wload(wc[:, :, 0:128], w_in, C_GA + j * 128, 128, "wc%d_0" % (j % 2), ("WC", j % 2, 0))
            wload(wc[:, :, 128:256], w_in, C_GB + j * 128, 128, "wc%d_1" % (j % 2), ("WC", j % 2, 1))
            wload(wc[:, :, 256:384], w_ba, j * 128, 128, "wc%d_2" % (j % 2), ("WC", j % 2, 2))
            wload(wc[:, :, 384:512], w_bh, j * 128, 128, "wc%d_3" % (j % 2), ("WC", j % 2, 3))
            for tg in range(NT // 4):
                i = it % 2
                it += 1
                bk = 4 * i
                tok0 = (WT + 4 * tg) * 128
                ms = slice(tg * 512, (tg + 1) * 512)
                proj_fm(psf(bk), wc[:, :, 0:128], [("WC", j % 2, 0)], tok0, 512, ("psC", bk))
                ACT([("psC", bk), "bT"], [("GAt", i)], out=GAt[i], in_=psf(bk), func=AF.Sigmoid,
                    bias=bT[:, C_GA // 128 + j:C_GA // 128 + j + 1], scale=1.0)
                proj_fm(psf(bk + 1), wc[:, :, 128:256], [("WC", j % 2, 1)], tok0, 512, ("psC", bk + 1))
                ACT([("psC", bk + 1), "bT"], [("GBt", i)], out=GBt[i], in_=psf(bk + 1), func=AF.Sigmoid,
                    bias=bT[:, C_GB // 128 + j:C_GB // 128 + j + 1], scale=1.0)
                for e_ in range(KC):
                    MM([("WC", j % 2, 2)] + YA_keys, [("psC", bk + 2)], psf(bk + 2), wc[:, e_, 256:384], YA[:, e_, ms], e_ == 0, e_ == KC - 1)
                OP("dve", "tensor_tensor", [("psC", bk + 2), ("GAt", i)], [("M1", i)], out=M1[i], in0=psf(bk + 2), in1=GAt[i], op=ALU.mult)
                for e_ in range(KC):
                    MM([("WC", j % 2, 3)] + YH_keys, [("psC", bk + 3)], psf(bk + 3), wc[:, e_, 384:512], YH[:, e_, ms], e_ == 0, e_ == KC - 1)
                OP("dve", "tensor_tensor", [("psC", bk + 3), ("GBt", i)], [("M2", i)], out=M2[i], in0=psf(bk + 3), in1=GBt[i], op=ALU.mult)
                OP("pool", "tensor_tensor", [("M1", i), ("M2", i)], [("MT", j, tg)], out=MT[:, j, ms], in0=M1[i], in1=M2[i], op=ALU.add)
        MT_keys = [("MT", j, tg) for j in range(8) for tg in range(NT // 4)]
        dump("MT", MT, KC, NTOK, MT_keys, R_UT)
        if stop_after == "C":
            return finish()

        P.barrier()
        H1 = A.f32(R_H1, NH, 1024)
        u2T = A.bf16(R_U2, KC, HTOK)
        ZT = A.bf16(R_ZT, FC, HTOK)
        xs2 = [A.f32(R_GAP + 4096 * i, 1024) for i in range(2)]
        td = Bump(R_TMP, R_END)
        WD = [A.bf16(td.take(FC * 256 * 2), FC, 256) for _ in range(2)]
        WO = A.bf16(R_TMP, KC, 1024)
        WGU = [A.bf16(td.take(KC * 256 * 2), KC, 256) for _ in range(3)]
        un2 = [A.bf16(td.take(2048), 1024) for _ in range(2)]
        SL = [A.bf16(td.take(HTOK * 2), HTOK) for _ in range(2)]
        GF = A.f32(td.take(4096), 1024)
        DMA("sp", "c_GF", [], ["GF"], GF, norm_final_g[0:1, :].partition_broadcast(128))
        gu = 0
        outd = []
        for hh in range(2):
            for c in range(2):
                DMA("pool", "wo%d" % c, [], [("WO", c), ("WD", 0), ("WD", 1)], WO[:, :, c * 512:(c + 1) * 512],
                    w_out.rearrange("(k p) c -> p k c", p=128)[:, :, c * 512:(c + 1) * 512])
            for tl in range(NH):
                mt = hh * NH + tl
                xb = xs2[tl % 2]
                kx = ("xs2", tl % 2)
                DMA("sp", "xs2_%d" % (tl % 2), [], [kx], xb, xext[(WT + mt) * 128:(WT + mt + 1) * 128, :])
                for c in range(2):
                    bank = (2 * tl + c) % 4
                    for e_ in range(KC):
                        MM(MT_keys + [("WO", c)], [("psD", bank)], psf(bank), MT[:, e_, mt * 128:(mt + 1) * 128], WO[:, e_, c * 512:(c + 1) * 512],
                           e_ == 0, e_ == KC - 1)
                    OP("dve", "tensor_tensor", [("psD", bank), kx], [("H1", tl, c)], out=H1[:, tl, c * 512:(c + 1) * 512], in0=psf(bank),
                       in1=xb[:, c * 512:(c + 1) * 512], op=ALU.add)
            for tl in range(NH):
                mt = hh * NH + tl
                ub = un2[tl % 2]
                ku = ("un2", tl % 2)
                hk = [("H1", tl, 0), ("H1", tl, 1)]
                ACT(hk, [("ss1", mt), ku], out=ub, in_=H1[:, tl, :], func=AF.Square, accum_out=ss1[:, mt:mt + 1])
                OP("dve", "tensor_scalar", [("ss1", mt)], [("ms1", mt)], out=ms1[:, mt:mt + 1], in0=ss1[:, mt:mt + 1], scalar1=1.0 / D,
                   scalar2=EPS, op0=ALU.mult, op1=ALU.add)
                OP("pool", "tensor_tensor", [("ms1", mt), "nhalf"], [("rs1", mt)], out=rs1[:, mt:mt + 1], in0=ms1[:, mt:mt + 1],
                   in1=nhalf[:, 0:1], op=ALU.pow)
                OP("dve", "tensor_scalar", hk + [("rs1", mt)], [ku], out=ub, in0=H1[:, tl, :], scalar1=rs1[:, mt:mt + 1], scalar2=None, op0=ALU.mult)
                bank = 4 + tl % 2
                pst = t3(psbf(bank), 128)
                for k in range(KC):
                    TR([ku, "ident"], [("psD", bank)], pst[:, k, :], ub[:, k * 128:(k + 1) * 128])
                OP("dve", "tensor_tensor", [("psD", bank), "gT2"], [("u2T", tl)], out=u2T[:, :, tl * 128:(tl + 1) * 128], in0=pst,
                   in1=bc_last(gT2, KC, 128), op=ALU.mult)
            u2_keys = [("u2T", tl) for tl in range(NH)]
            for j in range(FC):
                wg = WGU[gu % 3]
                wgk = ("WGU", gu % 3)
                wload(wg[:, :, 0:128], w_fg, j * 128, 128, "wgu%d_0" % (gu % 3), (wgk, 0))
                wload(wg[:, :, 128:256], w_fu, j * 128, 128, "wgu%d_1" % (gu % 3), (wgk, 1))
                sl_ = SL[gu % 2]
                slk = ("SL", gu % 2)
                bset = 4 * (gu % 2)
                gu += 1
                ci = 0
                for c in range(0, HTOK, 512):
                    n = min(512, HTOK - c)
                    bg, bu = bset + ci, bset + 2 + ci
                    ci += 1
                    for e_ in range(KC):
                        MM(u2_keys + [(wgk, 0)], [("psD", bg)], psf(bg, n), wg[:, e_, 0:128], u2T[:, e_, c:c + n], e_ == 0, e_ == KC - 1)
                    ACT([("psD", bg)], [(slk, c)], out=sl_[:, c:c + n], in_=psf(bg, n), func=AF.Silu)
                    for e_ in range(KC):
                        MM(u2_keys + [(wgk, 1)], [("psD", bu)], psf(bu, n), wg[:, e_, 128:256], u2T[:, e_, c:c + n], e_ == 0, e_ == KC - 1)
                    OP("dve", "tensor_tensor", [("psD", bu), (slk, c)], [("ZT", j, c)], out=ZT[:, j, c:c + n], in0=psf(bu, n), in1=sl_[:, c:c + n], op=ALU.mult)
            ZT_keys = [("ZT", j, c) for j in range(FC) for c in range(0, HTOK, 512)]
            for cq in range(4):
                wd = WD[cq % 2]
                wdk = ("WD", cq % 2)
                extra = [("WO", 0), ("WO", 1)]
                DMA("pool", "wd%d" % (cq % 2), extra, [wdk] + extra, wd,
                    w_fd.rearrange("(k p) c -> p k c", p=128)[:, :, cq * 256:(cq + 1) * 256])
                for tl in range(NH):
                    bank = (cq * NH + tl) % 4
                    for f in range(FC):
                        MM(ZT_keys + [wdk], [("psD", bank)], psf(bank, 256), ZT[:, f, tl * 128:(tl + 1) * 128], wd[:, f, :], f == 0, f == FC - 1)
                    hk = ("H1", tl, cq // 2)
                    OP("dve", "tensor_tensor", [("psD", bank), hk], [hk], out=H1[:, tl, cq * 256:(cq + 1) * 256], in0=psf(bank, 256),
                       in1=H1[:, tl, cq * 256:(cq + 1) * 256], op=ALU.add)
            for tl in range(NH):
                mt = hh * NH + tl
                ob = xs2[tl % 2]
                kx = ("xs2", tl % 2)
                hk = [("H1", tl, 0), ("H1", tl, 1)]
                ACT(hk, [("ss2", mt), kx], out=ob, in_=H1[:, tl, :], func=AF.Square, accum_out=ss2[:, mt:mt + 1])
                OP("dve", "tensor_scalar", [("ss2", mt)], [("ms2", mt)], out=ms2[:, mt:mt + 1], in0=ss2[:, mt:mt + 1], scalar1=1.0 / D,
                   scalar2=EPS, op0=ALU.mult, op1=ALU.add)
                OP("pool", "tensor_tensor", [("ms2", mt), "nhalf"], [("rs2", mt)], out=rs2[:, mt:mt + 1], in0=ms2[:, mt:mt + 1],
                   in1=nhalf[:, 0:1], op=ALU.pow)
                OP("dve", "scalar_tensor_tensor", hk + [("rs2", mt), "GF"], [kx], out=ob, in0=H1[:, tl, :], scalar=rs2[:, mt:mt + 1], in1=GF,
                   op0=ALU.mult, op1=ALU.mult)
                outd.append(DMA("sp", "out%d" % (tl % 2), [kx], [], out[mt * 128:(mt + 1) * 128, :], ob))
        return finish()


def make_in_maps(inputs, NT=16, WT=2, n_seq=4):
    x = np.ascontiguousarray(np.asarray(inputs["x"], dtype=np.float32))
    half_tok = NT * 128
    shared = {
        "norm_mix_g": inputs["norm_mix_g"].reshape(1, D),
        "w_in": inputs["w_in"].reshape(D, INW),
        "b_in": inputs["b_in"].reshape(1, INW),
        "attn_sinks": inputs["attn_sinks"].reshape(1, 16),
        "hgrn_lb_logits": inputs["hgrn_lb_logits"].reshape(2, D),
        "hgrn_norm_g": inputs["hgrn_norm_g"].reshape(1, D),
        "w_branch_attn": inputs["w_branch_attn"].reshape(D, D),
        "w_branch_hgrn": inputs["w_branch_hgrn"].reshape(D, D),
        "w_out": inputs["w_out"].reshape(D, D),
        "norm_ffn_g": inputs["norm_ffn_g"].reshape(1, D),
        "w_ffn_gate": inputs["w_ffn_gate"].reshape(D, FF),
        "w_ffn_up": inputs["w_ffn_up"].reshape(D, FF),
        "w_ffn_down": inputs["w_ffn_down"].reshape(FF, D),
        "norm_final_g": inputs["norm_final_g"].reshape(1, D),
    }
    shared = {k: np.ascontiguousarray(np.asarray(v, dtype=np.float32)) for k, v in shared.items()}
    maps = []
    for c in range(2 * n_seq):
        b, half = c // 2, c % 2
        xe = np.zeros(((WT + NT) * 128, D), np.float32)
        if half == 1:
            xe[:WT * 128] = x[b, half_tok - WT * 128: half_tok]
        xe[WT * 128:] = x[b, half * half_tok:(half + 1) * half_tok]
        fl = np.zeros((128, 2), np.float32)
        fl[:, 0] = float(half)
        m = dict(shared)
        m["xext"] = xe
        m["flags"] = fl
        maps.append(m)
    return maps


_CACHE = {}


def kernel(**inputs):
    NT, WT = 16, 2
    if "nc" not in _CACHE:
        _CACHE["nc"] = build(NT=NT, WT=WT)
    nc = _CACHE["nc"]
    maps = make_in_maps(inputs, NT=NT, WT=WT, n_seq=4)
    res = run_bass_kernel_spmd(nc, maps, core_ids=list(range(8)))
    half_tok = NT * 128
    out = np.empty((4, 2 * half_tok, D), np.float32)
    for c in range(8):
        out[c // 2, (c % 2) * half_tok:(c % 2 + 1) * half_tok] = res.results[c]["out"]
    return out
```

```python
import numpy as np
from contextlib import ExitStack
import concourse.bass as bass
import concourse.mybir as mybir
from concourse.bass_utils import run_bass_kernel_spmd

F32 = mybir.dt.float32
BF16 = mybir.dt.bfloat16
AF = mybir.ActivationFunctionType
ALU = mybir.AluOpType

D = 1024
KC = 8
INW = 7424
C_AQ, C_AK, C_AV, C_HQ, C_HF, C_HI, C_HG, C_GA, C_GB = 0, 1024, 1152, 1280, 2304, 3328, 4352, 5376, 6400
FF = 2816
FC = 22
EPS = 1e-6
ARENA_KB = 203


class StopBuild(Exception):
    pass


class _Op:
    __slots__ = ("eng", "emit", "deps", "sig", "signal", "dma_sem", "dma_val", "is_dma", "ndma")


class Prog:
    ENGS = ("pe", "act", "dve", "pool", "sp")

    def __init__(self, nc):
        self.nc = nc
        self.ops = []
        self.last_w = {}
        self.readers = {}
        self.dma_sems = {}
        self.last_eng = {}
        self.last_dma = {}
        self.bar = {}
        self.limit = None

    def _mk(self, eng, emit, r, w, extra):
        if self.limit is not None and len(self.ops) >= self.limit:
            return None
        op = _Op()
        op.eng = eng
        op.emit = emit
        op.sig = False
        op.signal = 0
        op.is_dma = False
        op.dma_sem = None
        op.dma_val = 0
        op.ndma = 1
        deps = set(extra)
        b = self.bar.pop(eng, None)
        if b:
            deps.update(b)
        for k in r:
            lw = self.last_w.get(k)
            if lw is not None:
                deps.add(lw)
        for k in w:
            lw = self.last_w.get(k)
            if lw is not None:
                deps.add(lw)
            for rd in self.readers.get(k, ()):
                deps.add(rd)
        deps.discard(op)
        if eng == "pe":
            deps = {d for d in deps if d.is_dma or d.eng != "pe"}
        op.deps = deps
        for k in r:
            self.readers.setdefault(k, []).append(op)
        for k in w:
            self.last_w[k] = op
            self.readers[k] = []
        self.ops.append(op)
        return op

    def op(self, eng, emit, r=(), w=(), deps=()):
        op = self._mk(eng, emit, r, w, deps)
        if op is None:
            return None
        self.last_eng[eng] = op
        return op

    def dma(self, eng, emit, slot, r=(), w=(), deps=(), n=1):
        op = self._mk(eng, emit, r, w, deps)
        if op is None:
            return None
        op.is_dma = True
        op.ndma = n
        st = self.dma_sems.setdefault(slot, [None, 0])
        st[1] += 16 * n
        op.dma_sem = slot
        op.dma_val = st[1]
        self.last_dma[slot] = op
        return op

    def barrier(self):
        allp = list(self.last_eng.values()) + list(self.last_dma.values())
        for e in self.ENGS:
            self.bar[e] = list(allp)

    def emit_all(self, block, stack):
        nc = self.nc
        ops = self.ops
        for op in ops:
            for d in op.deps:
                if not d.is_dma:
                    d.sig = True
        cnt = {e: 0 for e in self.ENGS}
        for op in ops:
            if op.sig and not op.is_dma:
                cnt[op.eng] += 1
                op.signal = cnt[op.eng]
        esem = {e: stack.enter_context(nc.semaphore("c_" + e)) for e in self.ENGS}
        for slot, st in self.dma_sems.items():
            st[0] = stack.enter_context(nc.semaphore("d_" + str(slot)))
        by_eng = {e: [o for o in ops if o.eng == e] for e in self.ENGS}

        def run(engname, e):
            waited = {}
            for op in by_eng[engname]:
                need = {}
                for d in op.deps:
                    if d.is_dma:
                        key = ("d", d.dma_sem)
                        val = d.dma_val
                    else:
                        key = ("c", d.eng)
                        val = d.signal
                    if val > need.get(key, 0):
                        need[key] = val
                for key, val in need.items():
                    if waited.get(key, 0) >= val:
                        continue
                    waited[key] = val
                    sem = esem[key[1]] if key[0] == "c" else self.dma_sems[key[1]][0]
                    e.wait_ge(sem, val)
                ins = op.emit(e)
                if op.is_dma:
                    sem = self.dma_sems[op.dma_sem][0]
                    if not isinstance(ins, (list, tuple)):
                        ins = [ins]
                    assert len(ins) == op.ndma, (len(ins), op.ndma)
                    for i in ins:
                        i.then_inc(sem, 16)
                elif op.sig:
                    ins.then_inc(esem[engname], 1)

        @block.tensor
        def _(e):
            run("pe", e)

        @block.scalar
        def _(e):
            run("act", e)

        @block.vector
        def _(e):
            run("dve", e)

        @block.gpsimd
        def _(e):
            run("pool", e)

        @block.sync
        def _(e):
            run("sp", e)


class Arena:
    def __init__(self, ap):
        self.ap = ap

    def f32(self, off_b, *shape):
        n = 1
        for s in shape:
            n *= s
        assert off_b % 4 == 0 and off_b + 4 * n <= ARENA_KB * 1024, (off_b, shape)
        v = self.ap[:, off_b // 4: off_b // 4 + n]
        return self._shape(v, shape)

    def bf16(self, off_b, *shape):
        n = 1
        for s in shape:
            n *= s
        assert off_b % 4 == 0 and n % 2 == 0 and off_b + 2 * n <= ARENA_KB * 1024, (off_b, shape)
        v = self.ap[:, off_b // 4: off_b // 4 + n // 2].bitcast(BF16)
        return self._shape(v, shape)

    @staticmethod
    def _shape(v, shape):
        if len(shape) == 1:
            return v
        if len(shape) == 2:
            return v.rearrange("p (a b) -> p a b", a=shape[0], b=shape[1])
        if len(shape) == 3:
            return v.rearrange("p (a b c) -> p a b c", a=shape[0], b=shape[1], c=shape[2])
        raise ValueError(shape)


class Bump:
    def __init__(self, start, end):
        self.p = start
        self.end = end

    def take(self, nbytes):
        nbytes = (nbytes + 31) // 32 * 32
        o = self.p
        self.p += nbytes
        assert self.p <= self.end, ("arena region overflow", self.p, self.end)
        return o


def build(NT=16, WT=2, dbg=(), stop_after=None):
    assert NT % 4 == 0 and WT >= 1 and WT <= 4
    TT = WT + NT
    NTOK = NT * 128
    TTOK = TT * 128
    NB = NT
    NH = NT // 2
    HTOK = NH * 128

    nc = bass.Bass("TRN2", target_bir_lowering=False)

    def din(name, shape):
        return nc.dram_tensor(name, list(shape), F32, kind="ExternalInput").ap()

    xext = din("xext", [TTOK, D])
    flags = din("flags", [128, 2])
    norm_mix_g = din("norm_mix_g", [1, D])
    w_in = din("w_in", [D, INW])
    b_in = din("b_in", [1, INW])
    attn_sinks = din("attn_sinks", [1, 16])
    lb_logits = din("hgrn_lb_logits", [2, D])
    hgrn_norm_g = din("hgrn_norm_g", [1, D])
    w_ba = din("w_branch_attn", [D, D])
    w_bh = din("w_branch_hgrn", [D, D])
    w_out = din("w_out", [D, D])
    norm_ffn_g = din("norm_ffn_g", [1, D])
    w_fg = din("w_ffn_gate", [D, FF])
    w_fu = din("w_ffn_up", [D, FF])
    w_fd = din("w_ffn_down", [FF, D])
    norm_final_g = din("norm_final_g", [1, D])
    out = nc.dram_tensor("out", [NTOK, D], F32, kind="ExternalOutput").ap()
    dbg_out = {}
    for nm, shp in dbg:
        dbg_out[nm] = nc.dram_tensor("dbg_" + nm, list(shp), F32, kind="ExternalOutput").ap()

    st = ExitStack()
    with st:
        arena_t = st.enter_context(nc.sbuf_tensor("arena", [128, ARENA_KB * 256], F32))
        A = Arena(arena_t[:])
        psb = [st.enter_context(nc.psum_tensor("ps%d" % i, [128, 512], F32)) for i in range(8)]

        def psf(i, n=512):
            return psb[i][:, 0:n]

        def psbf(i):
            return psb[i][:].bitcast(BF16)

        P = Prog(nc)
        import os as _os
        if _os.environ.get('CUT'):
            P.limit = int(_os.environ['CUT'])

        def OP(eng, method, r, w, *args, **kw):
            return P.op(eng, lambda e: getattr(e, method)(*args, **kw), r=r, w=w)

        def DMA(eng, slot, r, w, out, in_, **kw):
            return P.dma(eng, lambda e: e.dma_start(out=out, in_=in_, **kw), slot, r=r, w=w)

        def ACT(r, w, **kw):
            return OP("act", "activation", r, w, **kw)

        def MM(r, w, out, lhsT, rhs, start, stop):
            return OP("pe", "matmul", r, w, out, lhsT=lhsT, rhs=rhs, start=start, stop=stop)

        def TR(r, w, out, in_):
            return OP("pe", "transpose", r, w, out, in_, ident)

        def finish():
            P.limit = None
            fin = list(P.last_dma.values())
            P.op("sp", lambda e: None, deps=fin)
            print('n_ops', len(P.ops), 'n_sems', len(P.dma_sems) + 5)
            blk = st.enter_context(nc.Block())
            P.emit_all(blk, st)
            return nc

        def bc_mid(ap2d, a, c):
            return ap2d.rearrange("p (o c) -> p o c", o=1).to_broadcast([128, a, c])

        def bc_last(ap2d, a, c):
            return ap2d.rearrange("p (a o) -> p a o", o=1).to_broadcast([128, a, c])

        def t3(ap2d, c):
            return ap2d.rearrange("p (a c) -> p a c", c=c)

        cst = Bump(0, 20 * 1024)
        R_UT = 20 * 1024
        R_YH = R_UT + TTOK * KC * 2
        R_YA = R_YH + NTOK * KC * 2
        R_END = ARENA_KB * 1024
        R_TMP = R_END - 48 * 1024
        R_MT = R_TMP - NTOK * KC * 2
        assert R_YA + NTOK * KC * 2 <= R_MT
        R_H1 = R_UT
        R_U2 = R_H1 + NH * 4096
        R_ZT = R_U2 + HTOK * KC * 2
        R_GAP = R_ZT + FC * HTOK * 2
        assert R_GAP + 8192 <= R_MT, (R_GAP, R_MT)

        uT = A.bf16(R_UT, KC, TTOK)
        YH = A.bf16(R_YH, KC, NTOK)
        YA = A.bf16(R_YA, KC, NTOK)
        MT = A.bf16(R_MT, KC, NTOK)

        ident = A.bf16(cst.take(256), 128)
        cmask = A.bf16(cst.take(256), 128)
        maskc = A.bf16(cst.take(256), 128)
        maskp = A.bf16(cst.take(256), 128)
        maskp0 = A.bf16(cst.take(256), 128)
        onesev = A.bf16(cst.take(256), 128)
        onesod = A.bf16(cst.take(256), 128)
        RM = A.f32(cst.take(2048), 512)
        bT = A.f32(cst.take(58 * 4), 58)
        bkd = A.f32(cst.take(8), 2)
        BH = A.f32(cst.take(4096), 1024)
        BV2 = A.f32(cst.take(1024), 2, 128)
        gT1 = A.f32(cst.take(32), KC)
        gT2 = A.f32(cst.take(32), KC)
        ngT = A.f32(cst.take(32), 8)
        lgt = A.f32(cst.take(64), 2, 8)
        lbT = A.f32(cst.take(32), 8)
        omlT = A.f32(cst.take(32), 8)
        lsum = A.f32(cst.take(32), 8)
        c0T = A.f32(cst.take(32), 8)
        c1T = A.f32(cst.take(32), 8)
        bTh = A.f32(cst.take(58 * 4), 58)
        snk = A.f32(cst.take(64), 16)
        flg = A.f32(cst.take(8), 2)
        nhalf = A.f32(cst.take(64), 16)
        ss0 = A.f32(cst.take(4 * TT), TT)
        ms0 = A.f32(cst.take(4 * TT), TT)
        rs0 = A.f32(cst.take(4 * TT), TT)
        ss1 = A.f32(cst.take(4 * NT), NT)
        ms1 = A.f32(cst.take(4 * NT), NT)
        rs1 = A.f32(cst.take(4 * NT), NT)
        ss2 = A.f32(cst.take(4 * NT), NT)
        ms2 = A.f32(cst.take(4 * NT), NT)
        rs2 = A.f32(cst.take(4 * NT), NT)

        with nc.allow_non_contiguous_dma(reason="small one-time constant loads"):
            DMA("sp", "c_bT", [], ["bT"], bT, b_in.rearrange("o (j p) -> p (o j)", p=128), allow_slow_non_contiguous=True)
            DMA("sp", "c_g1", [], ["gT1"], gT1, norm_mix_g.rearrange("o (k p) -> p (o k)", p=128), allow_slow_non_contiguous=True)
            DMA("sp", "c_g2", [], ["gT2"], gT2, norm_ffn_g.rearrange("o (k p) -> p (o k)", p=128), allow_slow_non_contiguous=True)
            DMA("sp", "c_ng", [], ["ngT"], ngT, hgrn_norm_g.rearrange("o (h p) -> p (o h)", p=128), allow_slow_non_contiguous=True)
            DMA("sp", "c_lg", [], ["lgt"], lgt, lb_logits.rearrange("l (h p) -> p l h", p=128), allow_slow_non_contiguous=True)
            for g in range(2):
                for half in range(2):
                    DMA("sp", "c_bkd%d%d" % (g, half), [], [("bkd", g, half)], bkd[half * 64:(half + 1) * 64, g:g + 1],
                        b_in[0:1, C_AK + g * 64: C_AK + (g + 1) * 64].rearrange("o c -> c o"), allow_slow_non_contiguous=True)
        bkd_keys = [("bkd", g, half) for g in range(2) for half in range(2)]
        DMA("sp", "c_BH", [], ["BH"], BH, b_in[0:1, C_HI:C_HI + 1024].partition_broadcast(128))
        for g in range(2):
            for half in range(2):
                DMA("sp", "c_bv2%d%d" % (g, half), [], [("BV2", g, half)], BV2[:, g, half * 64:(half + 1) * 64],
                    b_in[0:1, C_AV + g * 64: C_AV + (g + 1) * 64].partition_broadcast(128))
        DMA("sp", "c_snk", [], ["snk"], snk, attn_sinks[0:1, :].partition_broadcast(128))
        DMA("sp", "c_flg", [], ["flg"], flg, flags)

        OP("pool", "memset", [], ["ident"], ident, 1.0)
        OP("pool", "affine_select", ["ident"], ["ident"], out=ident, in_=ident, pattern=[[-1, 128]], compare_op=ALU.is_equal,
           fill=0.0, base=0, channel_multiplier=1)
        OP("pool", "memset", [], ["maskc"], maskc, 1.0)
        OP("pool", "affine_select", ["maskc"], ["maskc"], out=maskc, in_=maskc, pattern=[[1, 128]], compare_op=ALU.is_ge,
           fill=0.0, base=0, channel_multiplier=-1)
        OP("pool", "memset", [], ["maskp"], maskp, 1.0)
        OP("pool", "affine_select", ["maskp"], ["maskp"], out=maskp, in_=maskp, pattern=[[-1, 128]], compare_op=ALU.is_gt,
           fill=0.0, base=0, channel_multiplier=1)
        OP("dve", "tensor_scalar", ["maskp", "flg"], ["maskp0"], out=maskp0, in0=maskp, scalar1=flg[:, 0:1], scalar2=None, op0=ALU.mult)
        OP("dve", "tensor_copy", ["maskc"], ["cmask"], out=cmask, in_=maskc)
        OP("dve", "memset", ["cmask"], ["cmask"], cmask[0:64, 64:128], 0.0)
        OP("dve", "memset", [], ["onesev"], onesev, 0.0)
        OP("dve", "memset", ["onesev"], ["onesev"], onesev[:, 0:64], 1.0)
        OP("dve", "memset", [], ["onesod"], onesod, 0.0)
        OP("dve", "memset", ["onesod"], ["onesod"], onesod[:, 64:128], 1.0)
        OP("dve", "memset", [], ["RM"], RM, 1.0)
        OP("dve", "memset", ["RM"], ["RM"], t3(RM, 64)[:, :, 0:1], 0.0)
        OP("dve", "memset", [], ["nhalf"], nhalf, -0.5)
        ACT(["lgt"], ["lgt"], out=lgt, in_=lgt, func=AF.Exp)
        OP("dve", "tensor_tensor", ["lgt"], ["lsum"], out=lsum, in0=lgt[:, 0, :], in1=lgt[:, 1, :], op=ALU.add)
        OP("dve", "reciprocal", ["lsum"], ["lsum"], out=lsum, in_=lsum)
        OP("dve", "tensor_tensor", ["lgt", "lsum"], ["lbT"], out=lbT, in0=lgt[:, 0, :], in1=lsum, op=ALU.mult)
        OP("dve", "tensor_scalar", ["lbT"], ["omlT"], out=omlT, in0=lbT, scalar1=-1.0, scalar2=1.0, op0=ALU.mult, op1=ALU.add)
        OP("dve", "tensor_scalar", ["omlT"], ["c1T"], out=c1T, in0=omlT, scalar1=0.5, scalar2=None, op0=ALU.mult)
        OP("dve", "tensor_tensor", ["lbT", "c1T"], ["c0T"], out=c0T, in0=lbT, in1=c1T, op=ALU.add)
        OP("dve", "tensor_scalar", ["bT"], ["bTh"], out=bTh, in0=bT, scalar1=0.5, scalar2=None, op0=ALU.mult)
        ACT(["snk"], ["snk"], out=snk, in_=snk, func=AF.Exp)

        if stop_after == "K":
            return finish()

        def wload(dst, src2d, c0, ncols, slot, key, r=()):
            srcv = src2d.rearrange("(k p) c -> p k c", p=128)[:, :, c0:c0 + ncols]
            return DMA("pool", slot, list(r), [key], dst, srcv)

        def uTk(t0, nt):
            return [("uT", t) for t in range(t0, t0 + nt)]

        def proj_fm(ps_ap, wt, wkeys, tok0, ntok, pskey):
            for k in range(KC):
                MM(list(wkeys) + uTk(tok0 // 128, ntok // 128), [pskey], ps_ap, wt[:, k, :], uT[:, k, tok0:tok0 + ntok],
                   k == 0, k == KC - 1)

        def dump(name, src3, nk, ncol, rkeys, tmp_off):
            if name not in dbg_out:
                return
            P.barrier()
            dtmp = [A.f32(tmp_off, ncol), A.f32(tmp_off + 4 * ncol, ncol)]
            for k in range(nk):
                OP("dve", "tensor_copy", list(rkeys), ["dtmp%d" % (k % 2)], out=dtmp[k % 2], in_=src3[:, k, :])
                DMA("sp", "dbg_%s_%d" % (name, k % 2), ["dtmp%d" % (k % 2)], [], dbg_out[name][k * 128:(k + 1) * 128, :], dtmp[k % 2])
            P.barrier()

        overlap0 = NTOK * KC >= 16 * 1024
        tmp0 = Bump(R_YH + NTOK * KC, R_YH + NTOK * KC * 2) if overlap0 else Bump(R_TMP, R_END)
        xs = [A.f32(tmp0.take(4096), 1024) for _ in range(3)]
        un = [A.bf16(tmp0.take(2048), 1024) for _ in range(2)]
        for t in range(TT):
            xb = xs[t % 3]
            ub = un[t % 2]
            kx, ku = ("xs", t % 3), ("un", t % 2)
            DMA("sp", "xs%d" % (t % 3), [], [kx], xb, xext[t * 128:(t + 1) * 128, :])
            ACT([kx], [("ss0", t), ku], out=ub, in_=xb, func=AF.Square, accum_out=ss0[:, t:t + 1])
            OP("dve", "tensor_scalar", [("ss0", t)], [("ms0", t)], out=ms0[:, t:t + 1], in0=ss0[:, t:t + 1], scalar1=1.0 / D,
               scalar2=EPS, op0=ALU.mult, op1=ALU.add)
            OP("pool", "tensor_tensor", [("ms0", t), "nhalf"], [("rs0", t)], out=rs0[:, t:t + 1], in0=ms0[:, t:t + 1],
               in1=nhalf[:, 0:1], op=ALU.pow)
            OP("dve", "tensor_scalar", [kx, ("rs0", t)], [ku], out=ub, in0=xb, scalar1=rs0[:, t:t + 1], scalar2=None, op0=ALU.mult)
            pst = t3(psbf(t % 2), 128)
            for k in range(KC):
                TR([ku, "ident"], [("psJ", t % 2)], pst[:, k, :], ub[:, k * 128:(k + 1) * 128])
            OP("dve", "tensor_tensor", [("psJ", t % 2), "gT1"], [("uT", t)], out=uT[:, :, t * 128:(t + 1) * 128], in0=pst,
               in1=bc_last(gT1, KC, 128), op=ALU.mult)
        dump("uT", uT, KC, TTOK, uTk(0, TT), R_YH)
        if stop_after == "0":
            return finish()

        if not overlap0:
            P.barrier()
        R_A = R_END - 115 * 1024
        assert R_A >= R_YA
        ta = Bump(R_A, R_END)
        HQ = 4
        WHQ = [A.bf16(ta.take(KC * 512 * 2), KC, 512) for _ in range(HQ)]
        S_f = [[A.f32(ta.take(512), 128) for _ in range(2)] for _ in range(HQ)]
        SSQ = A.f32(ta.take(64), HQ, 4)
        MSQ = A.f32(ta.take(64), HQ, 4)
        RSQ = A.f32(ta.take(64), HQ, 4)
        S_b = [[A.bf16(ta.take(256), 128) for _ in range(4)] for _ in range(HQ)]
        vflag = flg[:, 0:1]

        class TB:
            pass
        TS = []
        for _ in range(2):
            T_ = TB()
            for nm in ("F", "G", "KK", "B", "X1n", "XB", "Q"):
                setattr(T_, nm, A.f32(ta.take(2048), 512))
            T_.E2 = T_.G
            T_.E1 = T_.F
            TS.append(T_)
        HB = []
        for hq in range(HQ):
            b = TB()
            for nm in ("KLT", "QD", "KD", "QE", "GT"):
                setattr(b, nm, A.bf16(ta.take(1024), 512))
            b.KLa = A.bf16(ta.take(1024), 4, 128)
            b.KLb = A.bf16(ta.take(1024), 4, 128)
            b.V = A.bf16(ta.take(1024), 4, 128)
            b.ORAW = A.f32(ta.take(2048), 4, 128)
            b.ON = A.bf16(ta.take(1024), 4, 128)
            b.DEC = A.f32(ta.take(32), 8)
            b.Am = [A.bf16(ta.take(256), 128) for _ in range(2)]
            HB.append(b)
            OP("dve", "memset", [], [("KLa", hq), ("KLb", hq)], b.KLa, 0.0)
            OP("dve", "memset", [("KLb", hq)], [("KLb", hq)], b.KLb, 0.0)

        groups = []
        t = 0
        while t < WT:
            n = min(4, WT - t)
            groups.append((t, n, True))
            t += n
        for gi in range(NT // 4):
            groups.append((WT + 4 * gi, 4, False))

        PS_V, PS_T = 2, 3
        psT = psbf(PS_T)
        pjc = [0]

        PJ_BANKS = [0, 1, 4, 5, 6, 7]

        def pjbank():
            bnk = PJ_BANKS[pjc[0] % len(PJ_BANKS)]
            pjc[0] += 1
            return bnk, (("psJ", bnk) if bnk < 4 else ("psR", bnk - 4))

        def load_head_w(h):
            hq = h % HQ
            for ci, c0 in enumerate((C_HQ, C_HF, C_HI, C_HG)):
                wload(WHQ[hq][:, :, ci * 128:(ci + 1) * 128], w_in, c0 + h * 128, 128, "whq%d_%d" % (hq, ci), ("WHQ", hq, ci))

        pjsel = {}

        def S1a_pe(h, t0, nt, is_pre):
            hq = h % HQ
            b = HB[hq]
            wh = WHQ[hq]
            n = nt * 128
            tok0 = t0 * 128

            def K(nm):
                return (nm, hq)
            bk, bkey = pjbank()
            pjsel[(h, "F")] = (bk, bkey)
            proj_fm(psf(bk, n), wh[:, :, 128:256], [("WHQ", hq, 1)], tok0, n, bkey)
            for tl in range(nt):
                for k in range(KC):
                    MM([("WHQ", hq, 2), ("uT", t0 + tl)], ["psV"], psf(PS_V)[:, tl * 128:(tl + 1) * 128],
                       uT[:, k, (t0 + tl) * 128:(t0 + tl + 1) * 128], wh[:, k, 256:384], k == 0, k == KC - 1)
            for tl in range(nt):
                OP("dve", "tensor_tensor", ["psV", "BH"], [K(("V", tl))], out=b.V[:, tl, :], in0=psf(PS_V)[:, tl * 128:(tl + 1) * 128],
                   in1=BH[:, h * 128:(h + 1) * 128], op=ALU.add)
            if is_pre:
                return
            bk, bkey = pjbank()
            pjsel[(h, "Q")] = (bk, bkey)
            proj_fm(psf(bk, n), wh[:, :, 0:128], [("WHQ", hq, 0)], tok0, n, bkey)
            bk, bkey = pjbank()
            pjsel[(h, "G")] = (bk, bkey)
            proj_fm(psf(bk, n), wh[:, :, 384:512], [("WHQ", hq, 3)], tok0, n, bkey)

        def S1a_ev(h, t0, nt, is_pre):
            hq = h % HQ
            b = HB[hq]
            T = TS[h % 2]
            n = nt * 128

            def K(nm):
                return (nm, hq)

            def TK_(nm):
                return ("T", h % 2, nm)
            bk, bkey = pjsel[(h, "F")]
            cF = C_HF // 128 + h
            ACT([bkey, "bTh"], [TK_("F")], out=T.F[:, 0:n], in_=psf(bk, n), func=AF.Tanh, bias=bTh[:, cF:cF + 1], scale=0.5)
            if is_pre:
                return
            bk, bkey = pjsel[(h, "Q")]
            cQ = C_HQ // 128 + h
            ACT([bkey, "bTh"], [TK_("Q")], out=T.Q[:, 0:n], in_=psf(bk, n), func=AF.Tanh, bias=bTh[:, cQ:cQ + 1], scale=0.5)
            OP("dve", "tensor_scalar", [bkey, "bTh", TK_("Q")], [TK_("XB")], out=T.XB[:, 0:n], in0=psf(bk, n), scalar1=0.5,
               scalar2=bTh[:, cQ:cQ + 1], op0=ALU.mult, op1=ALU.add)
            OP("dve", "scalar_tensor_tensor", [TK_("Q"), TK_("XB")], [TK_("Q")], out=T.Q[:, 0:n], in0=T.Q[:, 0:n], scalar=1.0,
               in1=T.XB[:, 0:n], op0=ALU.add, op1=ALU.mult)
            bk, bkey = pjsel[(h, "G")]
            cG = C_HG // 128 + h
            ACT([bkey, "bTh"], [K("GT")], out=b.GT[:, 0:n], in_=psf(bk, n), func=AF.Tanh, bias=bTh[:, cG:cG + 1], scale=0.5)
            OP("pool", "tensor_scalar", [K("GT")], [K("GT")], out=b.GT[:, 0:n], in0=b.GT[:, 0:n], scalar1=0.5, scalar2=0.5,
               op0=ALU.mult, op1=ALU.add)

        def S1b(h, t0, nt, is_pre, preload):
            hq = h % HQ
            b = HB[hq]
            T = TS[h % 2]
            n = nt * 128
            nch = nt * 2

            def K(nm):
                return (nm, hq)

            def TK_(nm):
                return ("T", h % 2, nm)
            B3 = t3(T.B[:, 0:n], 64)
            OP("dve", "tensor_scalar", [TK_("F"), "c1T", "c0T"], [TK_("F")], out=T.F[:, 0:n], in0=T.F[:, 0:n],
               scalar1=c1T[:, h:h + 1], scalar2=c0T[:, h:h + 1], op0=ALU.mult, op1=ALU.add)
            ACT([TK_("F")], [TK_("G")], out=T.G[:, 0:n], in_=T.F[:, 0:n], func=AF.Ln)
            OP("pool", "tensor_scalar", [TK_("F")], [TK_("KK")], out=T.KK[:, 0:n], in0=T.F[:, 0:n], scalar1=-1.0, scalar2=1.0,
               op0=ALU.mult, op1=ALU.add)

        def S1b2(h, t0, nt, is_pre):
            hq = h % HQ
            b = HB[hq]
            T = TS[h % 2]
            n = nt * 128
            nch = nt * 2

            def K(nm):
                return (nm, hq)

            def TK_(nm):
                return ("T", h % 2, nm)
            B3 = t3(T.B[:, 0:n], 64)
            OP("dve", "tensor_tensor_scan", [TK_("G"), "RM"], [TK_("B")], out=T.B[:, 0:n], data0=RM[:, 0:n], data1=T.G[:, 0:n],
               initial=0.0, op0=ALU.mult, op1=ALU.add)
            OP("pool", "tensor_tensor", [TK_("B")], [TK_("G")], out=t3(T.E2[:, 0:n], 64),
               in0=B3[:, :, 63:64].to_broadcast([128, nch, 64]), in1=B3, op=ALU.subtract)
            if not is_pre:
                OP("pool", "tensor_tensor", [TK_("B")], [TK_("F")], out=t3(T.E1[:, 0:n], 64), in0=B3,
                   in1=B3[:, :, 31:32].to_broadcast([128, nch, 64]), op=ALU.subtract)
            ACT([TK_("G")], [TK_("G")], out=T.E2[:, 0:n], in_=T.E2[:, 0:n], func=AF.Exp)
            ACT([TK_("B")], [K("DEC")], out=b.DEC[:, 0:nch].rearrange("p (c o) -> p c o", o=1), in_=B3[:, :, 63:64], func=AF.Exp)
            if not is_pre:
                ACT([TK_("F")], [TK_("X1n")], out=T.X1n[:, 0:n], in_=T.E1[:, 0:n], func=AF.Exp, scale=-1.0)
                ACT([TK_("F")], [TK_("F")], out=T.E1[:, 0:n], in_=T.E1[:, 0:n], func=AF.Exp)
                ACT([TK_("B")], [TK_("XB")], out=T.XB[:, 0:n], in_=T.B[:, 0:n], func=AF.Exp)
            OP("dve", "tensor_tensor", [TK_("KK"), TK_("G")], [K("KLT")], out=b.KLT[:, 0:n], in0=T.KK[:, 0:n], in1=T.E2[:, 0:n], op=ALU.mult)
            if not is_pre:
                OP("dve", "tensor_tensor", [TK_("Q"), TK_("F")], [K("QD")], out=b.QD[:, 0:n], in0=T.Q[:, 0:n], in1=T.E1[:, 0:n], op=ALU.mult)
                OP("dve", "tensor_tensor", [TK_("KK"), TK_("X1n")], [K("KD")], out=b.KD[:, 0:n], in0=T.KK[:, 0:n], in1=T.X1n[:, 0:n], op=ALU.mult)
                OP("dve", "tensor_tensor", [TK_("Q"), TK_("XB")], [K("QE")], out=b.QE[:, 0:n], in0=T.Q[:, 0:n], in1=T.XB[:, 0:n], op=ALU.mult)
            for tl in range(nt):
                TR([K("KLT"), "ident"], ["psT"], psT[:, tl * 128:(tl + 1) * 128], b.KLT[:, tl * 128:(tl + 1) * 128])
            kt3 = t3(psT[:, 0:nt * 128], 128)
            ACT(["psT"], [K("KLa")], out=b.KLa[0:64, 0:nt, :], in_=kt3[0:64], func=AF.Copy)
            ACT(["psT"], [K("KLb")], out=b.KLb[64:128, 0:nt, :], in_=kt3[64:128], func=AF.Copy)

        chunk_ctr = [0] * HQ

        def S2_tile(heads, t0, tl, is_pre):
            par = tl % 2
            sl = slice(tl * 128, (tl + 1) * 128)
            info = []
            for h in heads:
                hq = h % HQ
                b = HB[hq]
                ps_r = psf(4 + hq)
                c0i = chunk_ctr[hq]
                info.append(dict(
                    h=h, hq=hq, b=b, AT=ps_r[:, 0:128], P0=ps_r[:, 128:256], P1=ps_r[:, 256:384], O=ps_r[:, 384:512], rk=("psR", hq),
                    sb0=S_b[hq][c0i % 4], sb1=S_b[hq][(c0i + 1) % 4], sb2=S_b[hq][(c0i + 2) % 4],
                    kb0=("S_b", hq, c0i % 4), kb1=("S_b", hq, (c0i + 1) % 4), kb2=("S_b", hq, (c0i + 2) % 4),
                    sA=S_f[hq][0], sB=S_f[hq][1], kA=("S_f", hq, 0), kB=("S_f", hq, 1)))

            def K(d, nm):
                return (nm, d["hq"])
            for d in info:
                b = d["b"]
                if not is_pre:
                    MM([K(d, "KD"), K(d, "QD")], [d["rk"]], d["AT"], b.KD[:, sl], b.QD[:, sl], True, True)
                MM([K(d, "KLa"), K(d, ("V", tl))], [d["rk"]], d["P0"], b.KLa[:, tl, :], b.V[:, tl, :], True, True)
                MM([K(d, "KLb"), K(d, ("V", tl))], [d["rk"]], d["P1"], b.KLb[:, tl, :], b.V[:, tl, :], True, True)
            for d in info:
                b = d["b"]
                OP("dve", "scalar_tensor_tensor", [d["kA"], K(d, "DEC"), d["rk"]], [d["kB"]], out=d["sB"], in0=d["sA"],
                   scalar=b.DEC[:, 2 * tl:2 * tl + 1], in1=d["P0"], op0=ALU.mult, op1=ALU.add)
            if not is_pre:
                for d in info:
                    ACT([d["kB"]], [d["kb1"]], out=d["sb1"], in_=d["sB"], func=AF.Copy)
            for d in info:
                b = d["b"]
                OP("dve", "scalar_tensor_tensor", [d["kB"], K(d, "DEC"), d["rk"]], [d["kA"]], out=d["sA"], in0=d["sB"],
                   scalar=b.DEC[:, 2 * tl + 1:2 * tl + 2], in1=d["P1"], op0=ALU.mult, op1=ALU.add)
                if is_pre and (t0 + tl == WT - 1):
                    OP("dve", "tensor_scalar", [d["kA"], "flg"], [d["kA"]], out=d["sA"], in0=d["sA"], scalar1=vflag, scalar2=None, op0=ALU.mult)
            if not is_pre:
                for d in info:
                    OP("dve", "tensor_tensor", [d["rk"], "cmask"], [K(d, ("Am", par))], out=d["b"].Am[par], in0=d["AT"], in1=cmask, op=ALU.mult)
            for d in info:
                ACT([d["kA"]], [d["kb2"]], out=d["sb2"], in_=d["sA"], func=AF.Copy)
            if not is_pre:
                for d in info:
                    b = d["b"]
                    O = d["O"]
                    MM([K(d, ("Am", par)), K(d, ("V", tl))], [d["rk"]], O, b.Am[par], b.V[:, tl, :], True, False)
                    MM([K(d, "QE"), d["kb0"]], [d["rk"]], O[0:64, :], b.QE[:, tl * 128:tl * 128 + 64], d["sb0"], False, True)
                    MM([K(d, "QE"), d["kb1"]], [d["rk"]], O[64:128, :], b.QE[:, tl * 128 + 64:tl * 128 + 128], d["sb1"], False, True)
                for d in info:
                    b = d["b"]
                    ACT([d["rk"]], [("SSQ", d["hq"], tl), K(d, ("ORAW", tl))], out=b.ORAW[:, tl, :], in_=d["O"], func=AF.Square,
                        accum_out=SSQ[:, d["hq"], tl:tl + 1])
                for d in info:
                    b = d["b"]
                    OP("dve", "tensor_copy", [d["rk"]], [K(d, ("ORAW", tl))], out=b.ORAW[:, tl, :], in_=d["O"])
                sk = [("SSQ", d["hq"], tl) for d in info]
                OP("dve", "tensor_scalar", sk, [("MSQ", tl)], out=MSQ[:, :, tl:tl + 1], in0=SSQ[:, :, tl:tl + 1],
                   scalar1=1.0 / 128.0, scalar2=EPS, op0=ALU.mult, op1=ALU.add)
                OP("pool", "tensor_tensor", [("MSQ", tl), "nhalf"], [("RSQ", tl)], out=RSQ[:, :, tl:tl + 1], in0=MSQ[:, :, tl:tl + 1],
                   in1=bc_last(nhalf[:, 0:HQ], HQ, 1), op=ALU.pow)
            for h in heads:
                chunk_ctr[h % HQ] += 2

        def S3_quad(heads, t0, nt):
            n = nt * 128
            nh = len(heads)
            for h in heads:
                hq = h % HQ
                b = HB[hq]
                OP("dve", "tensor_tensor", [(("ORAW", tl), hq) for tl in range(nt)] + [("RSQ", tl) for tl in range(nt)], [("ON", hq)], out=b.ON[:, 0:nt, :],
                   in0=b.ORAW[:, 0:nt, :], in1=bc_last(RSQ[:, hq, 0:nt], nt, 128), op=ALU.mult)
            mt0 = (t0 - WT) * 128
            for h in heads:
                hq = h % HQ
                b = HB[hq]
                if hq % 2 == 0:
                    dst, dkey = psT[:, 512:1024], "psT"
                else:
                    dst, dkey = psbf(PS_V)[:, 0:512], "psV"
                for tl in range(nt):
                    TR([("ON", hq), "ident"], [dkey], dst[:, tl * 128:(tl + 1) * 128], b.ON[:, tl, :])
                OP("dve", "scalar_tensor_tensor", [dkey, "ngT", ("GT", hq)], [("YH", h, mt0 // 512)], out=YH[:, h, mt0:mt0 + n],
                   in0=dst[:, 0:n], scalar=ngT[:, h:h + 1], in1=b.GT[:, 0:n], op0=ALU.mult, op1=ALU.add if False else ALU.mult)

        for q in range(8 // HQ):
            heads = list(range(q * HQ, (q + 1) * HQ))
            if q == 0:
                for h in heads:
                    load_head_w(h)
            for h in heads:
                hq = h % HQ
                OP("dve", "memset", [], [("S_f", hq, 0)], S_f[hq][0], 0.0)
                OP("dve", "memset", [], [("S_b", hq, 0)], S_b[hq][0], 0.0)
                chunk_ctr[hq] = 0
            for gi_, (t0, nt, is_pre) in enumerate(groups):
                pairs = [heads[i_:i_ + 2] for i_ in range(0, len(heads), 2)]
                last_g = (gi_ == len(groups) - 1 and q + 1 < 8 // HQ)

                def pe_stage(pair):
                    for h in pair:
                        S1a_pe(h, t0, nt, is_pre)
                        if last_g:
                            load_head_w(h + HQ)

                pe_stage(pairs[0])
                for pi_, pair in enumerate(pairs):
                    for h in pair:
                        S1a_ev(h, t0, nt, is_pre)
                    for h in pair:
                        S1b(h, t0, nt, is_pre, False)
                    if pi_ + 1 < len(pairs):
                        pe_stage(pairs[pi_ + 1])
                    for h in pair:
                        S1b2(h, t0, nt, is_pre)
                for tl in range(nt):
                    S2_tile(heads, t0, tl, is_pre)
                if not is_pre:
                    S3_quad(heads, t0, nt)

        YH_keys = [("YH", h, g) for h in range(8) for g in range(NT // 4)]
        dump("YH", YH, KC, NTOK, YH_keys, R_A)

        if stop_after == "A":
            return finish()

        P.barrier()
        R_AB = R_END - 80 * 1024
        assert R_YA + NTOK * KC * 2 <= R_AB
        tb_ = Bump(R_AB, R_END)
        WQ = [A.bf16(tb_.take(KC * 512 * 2), KC, 512) for _ in range(2)]
        WKV = [A.bf16(tb_.take(KC * 256 * 2), KC, 256) for _ in range(2)]
        QT = A.bf16(tb_.take(NB * 512 * 2), NB, 4, 128)
        KT = A.bf16(tb_.take((NB + 1) * 128 * 2), (NB + 1) * 128)
        Vev = A.bf16(tb_.take((NB + 1) * 128 * 2), NB + 1, 128)
        Vod = A.bf16(tb_.take((NB + 1) * 128 * 2), NB + 1, 128)
        PT = [[[A.bf16(tb_.take(1024), 512) for par in range(2)] for kb in range(2)] for buf in range(2)]
        ZS = [A.f32(tb_.take(2048), 512) for _ in range(2)]
        ESKC = A.f32(tb_.take(32), 8)
        MK = {nm: A.bf16(tb_.take(1024), 4, 128) for nm in ("maskc", "maskp", "maskp0")}
        BV2R = A.f32(tb_.take(4096), 2, 2, 4, 64) if False else None
        BVR = [[A.f32(tb_.take(1024), 4, 64) for half in range(2)] for g in range(2)]
        for g in range(2):
            for par in range(2):
                for pr in range(4):
                    hd = g * 8 + 2 * pr + par
                    OP("dve", "tensor_copy", ["snk"], [("ESK", g, par, pr)],
                       out=ESKC[par * 64:(par + 1) * 64, g * 4 + pr:g * 4 + pr + 1], in_=snk[par * 64:(par + 1) * 64, hd:hd + 1])
        ESK_keys = [[("ESK", g, par, pr) for par in range(2) for pr in range(4)] for g in range(2)]
        for nm, src in (("maskc", maskc), ("maskp", maskp), ("maskp0", maskp0)):
            for a_ in range(4):
                OP("dve", "tensor_copy", [nm], [("MK", nm, a_)], out=MK[nm][:, a_, :], in_=src)
        MK_keys = {nm: [("MK", nm, a_) for a_ in range(4)] for nm in MK}
        for g in range(2):
            for half in range(2):
                for a_ in range(4):
                    OP("dve", "tensor_copy", [("BV2", g, half)], [("BVR", g, half, a_)], out=BVR[g][half][:, a_, :],
                       in_=BV2[:, g, half * 64:(half + 1) * 64])
        OP("dve", "memset", [], ["Vev"], Vev, 0.0)
        OP("dve", "memset", [], ["Vod"], Vod, 0.0)
        KTOK0 = (WT - 1) * 128
        NKT = (NB + 1) * 128
        pj = 0
        for g in range(2):
            wq, wkv = WQ[g % 2], WKV[g % 2]
            wload(wq, w_in, C_AQ + g * 512, 512, "wq%d" % g, ("WQ", g))
            for i, c0 in enumerate((C_AK + g * 64, C_AK + g * 64, C_AV + g * 64, C_AV + g * 64)):
                wload(wkv[:, :, i * 64:(i + 1) * 64], w_in, c0, 64, "wkv%d_%d" % (g, i), ("WKV", g, i))
            for c in range(0, NKT, 512):
                n = min(512, NKT - c)
                bank = 6 + pj % 2
                pj += 1
                proj_fm(psf(bank, n), wkv[:, :, 0:128], [("WKV", g, 0), ("WKV", g, 1)], KTOK0 + c, n, ("psP", bank))
                ACT([("psP", bank)] + bkd_keys, [("KT", c // 512)], out=KT[:, c:c + n], in_=psf(bank, n), func=AF.Identity,
                    bias=bkd[:, g:g + 1], scale=1.0)
            KT_keys = [("KT", c // 512) for c in range(0, NKT, 512)]
            for j0 in range(0, NB + 1, 4):
                nb_ = min(4, NB + 1 - j0)
                bank = 6 + pj % 2
                pj += 1
                for jj in range(nb_):
                    te = WT - 1 + j0 + jj
                    for k in range(KC):
                        MM([("WKV", g, 2), ("WKV", g, 3), ("uT", te)], [("psP", bank)], psf(bank)[:, jj * 128:(jj + 1) * 128],
                           uT[:, k, te * 128:(te + 1) * 128], wkv[:, k, 128:256], k == 0, k == KC - 1)
                pv3 = t3(psf(bank, nb_ * 128), 128)
                OP("dve", "tensor_tensor", [("psP", bank), "Vev"] + [("BVR", g, 0, a_) for a_ in range(4)], [("Vev", j0 // 4)],
                   out=Vev[:, j0:j0 + nb_, 0:64], in0=pv3[:, :, 0:64], in1=BVR[g][0][:, 0:nb_, :], op=ALU.add)
                OP("dve", "tensor_tensor", [("psP", bank), "Vod"] + [("BVR", g, 1, a_) for a_ in range(4)], [("Vod", j0 // 4)],
                   out=Vod[:, j0:j0 + nb_, 64:128], in0=pv3[:, :, 64:128], in1=BVR[g][1][:, 0:nb_, :], op=ALU.add)
            V_keys = [("Vev", j // 4) for j in range(0, NB + 1, 4)] + [("Vod", j // 4) for j in range(0, NB + 1, 4)]
            for j in range(4):
                for tg in range(NT // 4):
                    bank = 6 + pj % 2
                    pj += 1
                    proj_fm(psf(bank), wq[:, :, j * 128:(j + 1) * 128], [("WQ", g)], (WT + 4 * tg) * 128, 512, ("psP", bank))
                    cq = C_AQ // 128 + g * 4 + j
                    ACT([("psP", bank), "bT"], [("QT", tg, j)], out=QT[:, 4 * tg:4 * tg + 4, j, :], in_=t3(psf(bank), 128),
                        func=AF.Identity, bias=bT[:, cq:cq + 1], scale=1.0)
            def stage1(nblk, g=g):
                buf = nblk % 2
                qkeys = [("QT", nblk // 4, j) for j in range(4)]
                for kb in range(2):
                    kblk = nblk + kb
                    mkey = "maskc" if kb == 1 else ("maskp0" if nblk == 0 else "maskp")
                    for par in range(2):
                        bank = kb * 2 + par
                        pt = PT[buf][kb][par]
                        ptk = ("PT", buf, kb, par)
                        MM(KT_keys + qkeys, [("psS", bank)], psf(bank), KT[par * 64:(par + 1) * 64, kblk * 128:(kblk + 1) * 128],
                           QT[par * 64:(par + 1) * 64, nblk, :, :].rearrange("p a c -> p (a c)"), True, True)
                        ACT([("psS", bank)], [ptk], out=pt, in_=psf(bank), func=AF.Exp, scale=0.125)
                        OP("pool", "tensor_tensor", [ptk] + MK_keys[mkey], [ptk], out=t3(pt, 128), in0=t3(pt, 128), in1=MK[mkey], op=ALU.mult)

            def stage2(nblk, g=g):
                buf = nblk % 2
                by, bz = (4, 5) if buf == 0 else (6, 7)
                ky, kz = (("psY" if buf == 0 else ("psP", 6))), (("psZ" if buf == 0 else ("psP", 7)))
                first = True
                for kb in range(2):
                    for par in range(2):
                        vt = (Vev if par == 0 else Vod)[:, nblk + kb, :]
                        last = (kb == 1 and par == 1)
                        MM(V_keys + [("PT", buf, kb, par)], [ky], psf(by), vt, PT[buf][kb][par], first, last)
                        MM(["onesev", "onesod", ("PT", buf, kb, par)], [kz], psf(bz), onesev if par == 0 else onesod, PT[buf][kb][par], first, last)
                        first = False
                zs = ZS[buf]
                for pr in range(4):
                    ACT([kz] + ESK_keys[g], [("ZS", buf, pr)], out=zs[:, pr * 128:(pr + 1) * 128], in_=psf(bz)[:, pr * 128:(pr + 1) * 128],
                        func=AF.Identity, bias=ESKC[:, g * 4 + pr:g * 4 + pr + 1], scale=1.0)
                OP("dve", "reciprocal", [("ZS", buf, pr) for pr in range(4)], [("ZS", buf)], out=zs, in_=zs)
                OP("dve", "tensor_tensor", [ky, ("ZS", buf)], [("YA", g, nblk)], out=YA[:, g * 4:(g + 1) * 4, nblk * 128:(nblk + 1) * 128],
                   in0=t3(psf(by), 128), in1=t3(zs, 128), op=ALU.mult)

            stage1(0)
            for nblk in range(NB):
                if nblk + 1 < NB:
                    stage1(nblk + 1)
                stage2(nblk)
        YA_keys = [("YA", g, n_) for g in range(2) for n_ in range(NB)]
        dump("YA", YA, KC, NTOK, YA_keys, R_AB)
        if stop_after == "B":
            return finish()

        P.barrier()
        tc_ = Bump(R_TMP, R_END)
        WC = [A.bf16(tc_.take(KC * 512 * 2), KC, 512) for _ in range(2)]
        GAt = [A.bf16(tc_.take(1024), 512) for _ in range(2)]
        GBt = [A.bf16(tc_.take(1024), 512) for _ in range(2)]
        M1 = [A.f32(tc_.take(2048), 512) for _ in range(2)]
        M2 = [A.f32(tc_.take(2048), 512) for _ in range(2)]
        it = 0
        for j in range(8):
            wc = WC[j % 2]
            wload(wc[:, :, 0:128], w_in, C_GA + j * 128, 128, "wc%d_0" % (j % 2), ("WC", j % 2, 0))
            wload(wc[:, :, 128:256], w_in, C_GB + j * 128, 128, "wc%d_1" % (j % 2), ("WC", j % 2, 1))
            wload(wc[:, :, 256:384], w_ba, j * 128, 128, "wc%d_2" % (j % 2), ("WC", j % 2, 2))
            wload(wc[:, :, 384:512], w_bh, j * 128, 128, "wc%d_3" % (j % 2), ("WC", j % 2, 3))
            for tg in range(NT // 4):
                i = it % 2
                it += 1
                bk = 4 * i
                tok0 = (WT + 4 * tg) * 128
                ms = slice(tg * 512, (tg + 1) * 512)
                proj_fm(psf(bk), wc[:, :, 0:128], [("WC", j % 2, 0)], tok0, 512, ("psC", bk))
                ACT([("psC", bk), "bT"], [("GAt", i)], out=GAt[i], in_=psf(bk), func=AF.Sigmoid,
                    bias=bT[:, C_GA // 128 + j:C_GA // 128 + j + 1], scale=1.0)
                proj_fm(psf(bk + 1), wc[:, :, 128:256], [("WC", j % 2, 1)], tok0, 512, ("psC", bk + 1))
                ACT([("psC", bk + 1), "bT"], [("GBt", i)], out=GBt[i], in_=psf(bk + 1), func=AF.Sigmoid,
                    bias=bT[:, C_GB // 128 + j:C_GB // 128 + j + 1], scale=1.0)
                for e_ in range(KC):
                    MM([("WC", j % 2, 2)] + YA_keys, [("psC", bk + 2)], psf(bk + 2), wc[:, e_, 256:384], YA[:, e_, ms], e_ == 0, e_ == KC - 1)
                OP("dve", "tensor_tensor", [("psC", bk + 2), ("GAt", i)], [("M1", i)], out=M1[i], in0=psf(bk + 2), in1=GAt[i], op=ALU.mult)
                for e_ in range(KC):
                    MM([("WC", j % 2, 3)] + YH_keys, [("psC", bk + 3)], psf(bk + 3), wc[:, e_, 384:512], YH[:, e_, ms], e_ == 0, e_ == KC - 1)
                OP("dve", "tensor_tensor", [("psC", bk + 3), ("GBt", i)], [("M2", i)], out=M2[i], in0=psf(bk + 3), in1=GBt[i], op=ALU.mult)
                OP("pool", "tensor_tensor", [("M1", i), ("M2", i)], [("MT", j, tg)], out=MT[:, j, ms], in0=M1[i], in1=M2[i], op=ALU.add)
        MT_keys = [("MT", j, tg) for j in range(8) for tg in range(NT // 4)]
        dump("MT", MT, KC, NTOK, MT_keys, R_UT)
        if stop_after == "C":
            return finish()

        P.barrier()
        H1 = A.f32(R_H1, NH, 1024)
        u2T = A.bf16(R_U2, KC, HTOK)
        ZT = A.bf16(R_ZT, FC, HTOK)
        xs2 = [A.f32(R_GAP + 4096 * i, 1024) for i in range(2)]
        td = Bump(R_TMP, R_END)
        WD = [A.bf16(td.take(FC * 256 * 2), FC, 256) for _ in range(2)]
        WO = A.bf16(R_TMP, KC, 1024)
        WGU = [A.bf16(td.take(KC * 256 * 2), KC, 256) for _ in range(3)]
        un2 = [A.bf16(td.take(2048), 1024) for _ in range(2)]
        SL = [A.bf16(td.take(HTOK * 2), HTOK) for _ in range(2)]
        GF = A.f32(td.take(4096), 1024)
        DMA("sp", "c_GF", [], ["GF"], GF, norm_final_g[0:1, :].partition_broadcast(128))
        gu = 0
        outd = []
        for hh in range(2):
            for c in range(2):
                DMA("pool", "wo%d" % c, [], [("WO", c), ("WD", 0), ("WD", 1)], WO[:, :, c * 512:(c + 1) * 512],
                    w_out.rearrange("(k p) c -> p k c", p=128)[:, :, c * 512:(c + 1) * 512])
            for tl in range(NH):
                mt = hh * NH + tl
                xb = xs2[tl % 2]
                kx = ("xs2", tl % 2)
                DMA("sp", "xs2_%d" % (tl % 2), [], [kx], xb, xext[(WT + mt) * 128:(WT + mt + 1) * 128, :])
                for c in range(2):
                    bank = (2 * tl + c) % 4
                    for e_ in range(KC):
                        MM(MT_keys + [("WO", c)], [("psD", bank)], psf(bank), MT[:, e_, mt * 128:(mt + 1) * 128], WO[:, e_, c * 512:(c + 1) * 512],
                           e_ == 0, e_ == KC - 1)
                    OP("dve", "tensor_tensor", [("psD", bank), kx], [("H1", tl, c)], out=H1[:, tl, c * 512:(c + 1) * 512], in0=psf(bank),
                       in1=xb[:, c * 512:(c + 1) * 512], op=ALU.add)
            for tl in range(NH):
                mt = hh * NH + tl
                ub = un2[tl % 2]
                ku = ("un2", tl % 2)
                hk = [("H1", tl, 0), ("H1", tl, 1)]
                ACT(hk, [("ss1", mt), ku], out=ub, in_=H1[:, tl, :], func=AF.Square, accum_out=ss1[:, mt:mt + 1])
                OP("dve", "tensor_scalar", [("ss1", mt)], [("ms1", mt)], out=ms1[:, mt:mt + 1], in0=ss1[:, mt:mt + 1], scalar1=1.0 / D,
                   scalar2=EPS, op0=ALU.mult, op1=ALU.add)
                OP("pool", "tensor_tensor", [("ms1", mt), "nhalf"], [("rs1", mt)], out=rs1[:, mt:mt + 1], in0=ms1[:, mt:mt + 1],
                   in1=nhalf[:, 0:1], op=ALU.pow)
                OP("dve", "tensor_scalar", hk + [("rs1", mt)], [ku], out=ub, in0=H1[:, tl, :], scalar1=rs1[:, mt:mt + 1], scalar2=None, op0=ALU.mult)
                bank = 4 + tl % 2
                pst = t3(psbf(bank), 128)
                for k in range(KC):
                    TR([ku, "ident"], [("psD", bank)], pst[:, k, :], ub[:, k * 128:(k + 1) * 128])
                OP("dve", "tensor_tensor", [("psD", bank), "gT2"], [("u2T", tl)], out=u2T[:, :, tl * 128:(tl + 1) * 128], in0=pst,
                   in1=bc_last(gT2, KC, 128), op=ALU.mult)
            u2_keys = [("u2T", tl) for tl in range(NH)]
            for j in range(FC):
                wg = WGU[gu % 3]
                wgk = ("WGU", gu % 3)
                wload(wg[:, :, 0:128], w_fg, j * 128, 128, "wgu%d_0" % (gu % 3), (wgk, 0))
                wload(wg[:, :, 128:256], w_fu, j * 128, 128, "wgu%d_1" % (gu % 3), (wgk, 1))
                sl_ = SL[gu % 2]
                slk = ("SL", gu % 2)
                bset = 4 * (gu % 2)
                gu += 1
                ci = 0
                for c in range(0, HTOK, 512):
                    n = min(512, HTOK - c)
                    bg, bu = bset + ci, bset + 2 + ci
                    ci += 1
                    for e_ in range(KC):
                        MM(u2_keys + [(wgk, 0)], [("psD", bg)], psf(bg, n), wg[:, e_, 0:128], u2T[:, e_, c:c + n], e_ == 0, e_ == KC - 1)
                    ACT([("psD", bg)], [(slk, c)], out=sl_[:, c:c + n], in_=psf(bg, n), func=AF.Silu)
                    for e_ in range(KC):
                        MM(u2_keys + [(wgk, 1)], [("psD", bu)], psf(bu, n), wg[:, e_, 128:256], u2T[:, e_, c:c + n], e_ == 0, e_ == KC - 1)
                    OP("dve", "tensor_tensor", [("psD", bu), (slk, c)], [("ZT", j, c)], out=ZT[:, j, c:c + n], in0=psf(bu, n), in1=sl_[:, c:c + n], op=ALU.mult)
            ZT_keys = [("ZT", j, c) for j in range(FC) for c in range(0, HTOK, 512)]
            for cq in range(4):
                wd = WD[cq % 2]
                wdk = ("WD", cq % 2)
                extra = [("WO", 0), ("WO", 1)]
                DMA("pool", "wd%d" % (cq % 2), extra, [wdk] + extra, wd,
                    w_fd.rearrange("(k p) c -> p k c", p=128)[:, :, cq * 256:(cq + 1) * 256])
                for tl in range(NH):
                    bank = (cq * NH + tl) % 4
                    for f in range(FC):
                        MM(ZT_keys + [wdk], [("psD", bank)], psf(bank, 256), ZT[:, f, tl * 128:(tl + 1) * 128], wd[:, f, :], f == 0, f == FC - 1)
                    hk = ("H1", tl, cq // 2)
                    OP("dve", "tensor_tensor", [("psD", bank), hk], [hk], out=H1[:, tl, cq * 256:(cq + 1) * 256], in0=psf(bank, 256),
                       in1=H1[:, tl, cq * 256:(cq + 1) * 256], op=ALU.add)
            for tl in range(NH):
                mt = hh * NH + tl
                ob = xs2[tl % 2]
                kx = ("xs2", tl % 2)
                hk = [("H1", tl, 0), ("H1", tl, 1)]
                ACT(hk, [("ss2", mt), kx], out=ob, in_=H1[:, tl, :], func=AF.Square, accum_out=ss2[:, mt:mt + 1])
                OP("dve", "tensor_scalar", [("ss2", mt)], [("ms2", mt)], out=ms2[:, mt:mt + 1], in0=ss2[:, mt:mt + 1], scalar1=1.0 / D,
                   scalar2=EPS, op0=ALU.mult, op1=ALU.add)
                OP("pool", "tensor_tensor", [("ms2", mt), "nhalf"], [("rs2", mt)], out=rs2[:, mt:mt + 1], in0=ms2[:, mt:mt + 1],
                   in1=nhalf[:, 0:1], op=ALU.pow)
                OP("dve", "scalar_tensor_tensor", hk + [("rs2", mt), "GF"], [kx], out=ob, in0=H1[:, tl, :], scalar=rs2[:, mt:mt + 1], in1=GF,
                   op0=ALU.mult, op1=ALU.mult)
                outd.append(DMA("sp", "out%d" % (tl % 2), [kx], [], out[mt * 128:(mt + 1) * 128, :], ob))
        return finish()


def make_in_maps(inputs, NT=16, WT=2, n_seq=4):
    x = np.ascontiguousarray(np.asarray(inputs["x"], dtype=np.float32))
    half_tok = NT * 128
    shared = {
        "norm_mix_g": inputs["norm_mix_g"].reshape(1, D),
        "w_in": inputs["w_in"].reshape(D, INW),
        "b_in": inputs["b_in"].reshape(1, INW),
        "attn_sinks": inputs["attn_sinks"].reshape(1, 16),
        "hgrn_lb_logits": inputs["hgrn_lb_logits"].reshape(2, D),
        "hgrn_norm_g": inputs["hgrn_norm_g"].reshape(1, D),
        "w_branch_attn": inputs["w_branch_attn"].reshape(D, D),
        "w_branch_hgrn": inputs["w_branch_hgrn"].reshape(D, D),
        "w_out": inputs["w_out"].reshape(D, D),
        "norm_ffn_g": inputs["norm_ffn_g"].reshape(1, D),
        "w_ffn_gate": inputs["w_ffn_gate"].reshape(D, FF),
        "w_ffn_up": inputs["w_ffn_up"].reshape(D, FF),
        "w_ffn_down": inputs["w_ffn_down"].reshape(FF, D),
        "norm_final_g": inputs["norm_final_g"].reshape(1, D),
    }
    shared = {k: np.ascontiguousarray(np.asarray(v, dtype=np.float32)) for k, v in shared.items()}
    maps = []
    for c in range(2 * n_seq):
        b, half = c // 2, c % 2
        xe = np.zeros(((WT + NT) * 128, D), np.float32)
        if half == 1:
            xe[:WT * 128] = x[b, half_tok - WT * 128: half_tok]
        xe[WT * 128:] = x[b, half * half_tok:(half + 1) * half_tok]
        fl = np.zeros((128, 2), np.float32)
        fl[:, 0] = float(half)
        m = dict(shared)
        m["xext"] = xe
        m["flags"] = fl
        maps.append(m)
    return maps


_CACHE = {}


def kernel(**inputs):
    NT, WT = 16, 2
    if "nc" not in _CACHE:
        _CACHE["nc"] = build(NT=NT, WT=WT)
    nc = _CACHE["nc"]
    maps = make_in_maps(inputs, NT=NT, WT=WT, n_seq=4)
    res = run_bass_kernel_spmd(nc, maps, core_ids=list(range(8)))
    half_tok = NT * 128
    out = np.empty((4, 2 * half_tok, D), np.float32)
    for c in range(8):
        out[c // 2, (c % 2) * half_tok:(c % 2 + 1) * half_tok] = res.results[c]["out"]
    return out
```
